# Optimizing a Trainium2 kernel written in Bass

```python
import jax, jax.numpy as jnp
from jax import lax
import numpy as np

D_MODEL = 2048
BATCH = 8
SEQ = 2048
DEPTH = 1
DEC_BATCH = 4
DEC_SEQ = 8192
PAST_LEN = 128

MIX_WIDTH = D_MODEL
POOL_WIDTH = MIX_WIDTH // 2
CONV_WIDTH = MIX_WIDTH - POOL_WIDTH
POOL_WINDOWS = (2, 4, 8, 16)
N_POOL_GROUPS = len(POOL_WINDOWS)
POOL_GROUP = POOL_WIDTH // N_POOL_GROUPS
CONV_HEADS = 8
CONV_K = 3
IN_WIDTH = POOL_WIDTH + 3 * CONV_WIDTH
D_FF = ((8 * D_MODEL // 3 + 127) // 128) * 128
N_SUBLAYERS = 3
N_MOD = 3 * N_SUBLAYERS
ALPHA = (2.0 * DEPTH) ** 0.25
BETA = (8.0 * DEPTH) ** -0.25
LN_EPS = 1e-5

kernel_name = "hybrid_pool_shortconv_macaron_adaln_encoder"


def layer_norm(x, g, b):
    xf = x.astype(jnp.float32)
    mu = jnp.mean(xf, axis=-1, keepdims=True)
    xc = xf - mu
    var = jnp.mean(jnp.square(xc), axis=-1, keepdims=True)
    y = xc * lax.rsqrt(var + LN_EPS)
    return (y * g.astype(jnp.float32) + b.astype(jnp.float32)).astype(x.dtype)


def modulate(x, shift, scale):
    return x * (1.0 + scale[:, None, :]) + shift[:, None, :]


def swiglu(h, w1, w3, w2):
    return (jax.nn.silu(h @ w1) * (h @ w3)) @ w2


def centred_mean_minus_self(z, window):
    bsz, seq, ch = z.shape
    zf = z.astype(jnp.float32)
    cs = jnp.concatenate([jnp.zeros((bsz, 1, ch), jnp.float32), jnp.cumsum(zf, axis=1)], axis=1)
    t = np.arange(seq)
    lo = np.clip(t - window // 2, 0, seq)
    hi = np.clip(t + window // 2, 0, seq)
    cnt = (hi - lo).astype(np.float32)
    mean = (cs[:, hi] - cs[:, lo]) / cnt[None, :, None]
    return (mean - zf).astype(z.dtype)


def pool_mixer(u, w_pool, s_pool):
    bsz, seq, _ = u.shape
    ug = u.reshape(bsz, seq, N_POOL_GROUPS, POOL_GROUP)
    pooled = jnp.stack([centred_mean_minus_self(ug[:, :, g], POOL_WINDOWS[g]) for g in range(N_POOL_GROUPS)], axis=2)
    pooled = jnp.einsum('bsgc,gcd->bsgd', pooled, w_pool)
    return pooled.reshape(bsz, seq, POOL_WIDTH) * s_pool


def short_conv_mixer(b_gate, c_gate, v, w_conv):
    seq = v.shape[1]
    z = c_gate * v
    pad = CONV_K // 2
    zp = jnp.pad(z, ((0, 0), (pad, pad), (0, 0)))
    y = zp[:, 0:seq] * w_conv[0]
    for k in range(1, CONV_K):
        y = y + zp[:, k:k + seq] * w_conv[k]
    return b_gate * y


def encoder_layer(x, c, w_ada, b_ada, ffn1_w1, ffn1_w3, ffn1_w2, w_in, w_pool, s_pool,
                  w_conv, w_out, ffn2_w1, ffn2_w3, ffn2_w2, ln_g, ln_b):
    bsz = x.shape[0]
    mod = (jax.nn.silu(c) @ w_ada + b_ada).reshape(bsz, N_MOD, D_MODEL)
    sh1, sc1, g1, sh2, sc2, g2, sh3, sc3, g3 = [mod[:, i] for i in range(N_MOD)]

    f1 = swiglu(modulate(x, sh1, sc1), ffn1_w1, ffn1_w3, ffn1_w2)
    x = layer_norm(ALPHA * x + 0.5 * (1.0 + g1)[:, None, :] * f1, ln_g[0], ln_b[0])

    proj = modulate(x, sh2, sc2) @ w_in
    u_pool = proj[..., :POOL_WIDTH]
    b_gate = proj[..., POOL_WIDTH:POOL_WIDTH + CONV_WIDTH]
    c_gate = proj[..., POOL_WIDTH + CONV_WIDTH:POOL_WIDTH + 2 * CONV_WIDTH]
    v = proj[..., POOL_WIDTH + 2 * CONV_WIDTH:]
    heads = jnp.concatenate([pool_mixer(u_pool, w_pool, s_pool),
                             short_conv_mixer(b_gate, c_gate, v, w_conv)], axis=-1)
    m = heads @ w_out
    x = layer_norm(ALPHA * x + (1.0 + g2)[:, None, :] * m, ln_g[1], ln_b[1])

    f2 = swiglu(modulate(x, sh3, sc3), ffn2_w1, ffn2_w3, ffn2_w2)
    x = layer_norm(ALPHA * x + 0.5 * (1.0 + g3)[:, None, :] * f2, ln_g[2], ln_b[2])
    return x


def trunk(x, c, w_ada, b_ada, ffn1_w1, ffn1_w3, ffn1_w2, w_in, w_pool, s_pool,
          w_conv, w_out, ffn2_w1, ffn2_w3, ffn2_w2, ln_g, ln_b):
    for l in range(DEPTH):
        x = encoder_layer(x, c, w_ada[l], b_ada[l], ffn1_w1[l], ffn1_w3[l], ffn1_w2[l],
                          w_in[l], w_pool[l], s_pool[l], w_conv[l], w_out[l],
                          ffn2_w1[l], ffn2_w3[l], ffn2_w2[l], ln_g[l], ln_b[l])
    return x


def setup_inputs(seed: int = 0) -> dict:
    key = jax.random.key(seed)
    ks = jax.random.split(key, 20)
    f32 = jnp.float32
    nrm = lambda k, shape, std: jax.random.normal(k, shape, f32) * std
    return {
        "x_prompt": nrm(ks[0], (BATCH, SEQ, D_MODEL), 1.0),
        "x_sample": nrm(ks[1], (DEC_BATCH, DEC_SEQ, D_MODEL), 1.0),
        "c_prompt": nrm(ks[2], (BATCH, D_MODEL), 1.0),
        "c_sample": nrm(ks[3], (DEC_BATCH, D_MODEL), 1.0),
        "w_ada": nrm(ks[4], (DEPTH, D_MODEL, N_MOD * D_MODEL), 0.5 * D_MODEL ** -0.5),
        "b_ada": nrm(ks[5], (DEPTH, N_MOD * D_MODEL), 0.01),
        "ffn1_w1": nrm(ks[6], (DEPTH, D_MODEL, D_FF), D_MODEL ** -0.5),
        "ffn1_w3": nrm(ks[7], (DEPTH, D_MODEL, D_FF), D_MODEL ** -0.5),
        "ffn1_w2": nrm(ks[8], (DEPTH, D_FF, D_MODEL), BETA * D_FF ** -0.5),
        "w_in": nrm(ks[9], (DEPTH, D_MODEL, IN_WIDTH), D_MODEL ** -0.5),
        "w_pool": nrm(ks[10], (DEPTH, N_POOL_GROUPS, POOL_GROUP, POOL_GROUP), POOL_GROUP ** -0.5),
        "s_pool": 1.0 + nrm(ks[11], (DEPTH, POOL_WIDTH), 0.02),
        "w_conv": nrm(ks[12], (DEPTH, CONV_K, CONV_WIDTH), CONV_K ** -0.5),
        "w_out": nrm(ks[13], (DEPTH, MIX_WIDTH, D_MODEL), BETA * MIX_WIDTH ** -0.5),
        "ffn2_w1": nrm(ks[14], (DEPTH, D_MODEL, D_FF), D_MODEL ** -0.5),
        "ffn2_w3": nrm(ks[15], (DEPTH, D_MODEL, D_FF), D_MODEL ** -0.5),
        "ffn2_w2": nrm(ks[16], (DEPTH, D_FF, D_MODEL), BETA * D_FF ** -0.5),
        "ln_g": 1.0 + nrm(ks[17], (DEPTH, N_SUBLAYERS, D_MODEL), 0.02),
        "ln_b": nrm(ks[18], (DEPTH, N_SUBLAYERS, D_MODEL), 0.02),
    }


def reference(x_prompt, x_sample, c_prompt, c_sample, w_ada, b_ada, ffn1_w1, ffn1_w3, ffn1_w2,
              w_in, w_pool, s_pool, w_conv, w_out, ffn2_w1, ffn2_w3, ffn2_w2, ln_g, ln_b):
    y_prompt = trunk(x_prompt, c_prompt, w_ada, b_ada, ffn1_w1, ffn1_w3, ffn1_w2, w_in, w_pool,
                     s_pool, w_conv, w_out, ffn2_w1, ffn2_w3, ffn2_w2, ln_g, ln_b)
    y_sample = trunk(x_sample, c_sample, w_ada, b_ada, ffn1_w1, ffn1_w3, ffn1_w2, w_in, w_pool,
                     s_pool, w_conv, w_out, ffn2_w1, ffn2_w3, ffn2_w2, ln_g, ln_b)
    return (y_prompt, y_sample)
```

```python
import os
import numpy as np
from contextlib import ExitStack

import concourse.bass as bass
import concourse.mybir as mybir
from concourse.bass_utils import run_bass_kernel_spmd

F32 = mybir.dt.float32
BF16 = mybir.dt.bfloat16
AF = mybir.ActivationFunctionType
ALU = mybir.AluOpType

D = 2048
DC = D // 128
FF = 5504
FC = FF // 128
PW = 1024
PC = PW // 128
IN_W = 4096
N_MOD = 9
ALPHA = 2.0 ** 0.25
LN_EPS = 1e-5
EPS_P = LN_EPS / (ALPHA * ALPHA)
TT = 512
FW = 528
WINDOWS = (2, 4, 8, 16)
KH = (22, 21)
NRING = 5
SLOT = 4096
NSCR = 10
NBANK = 7


class Ev:
    __slots__ = ("sem", "val")

    def __init__(self, sem, val):
        self.sem = sem
        self.val = val


class Res:
    __slots__ = ("W", "R", "ap")

    def __init__(self, ap=None):
        self.W = {}
        self.R = {}
        self.ap = ap


class Prog:
    def __init__(self):
        self.ops = {}
        self.cnt = {}
        self.known = {}
        self.evimp = {}
        self.evidx = {}
        self.n = 0

    def emit(self, eng, fn, reads=(), writes=(), deps=(), dma_sem=None):
        need = {}

        def add(d):
            for s, v in d.items():
                if v > need.get(s, 0):
                    need[s] = v
        for r in reads:
            add(r.W)
        for w in writes:
            add(w.W)
            add(w.R)
        for ev in deps:
            if ev is not None:
                add({ev.sem: ev.val})
        kn = self.known.setdefault(eng, {})
        waits = []
        for s, v in sorted(need.items(), key=lambda sv: -self.evidx.get(sv, 0)):
            if eng == "pe" and s == "pe":
                continue
            if v <= kn.get(s, 0):
                continue
            waits.append((s, v))
            kn[s] = v
            for s2, v2 in self.evimp.get((s, v), {}).items():
                if v2 > kn.get(s2, 0):
                    kn[s2] = v2
        if dma_sem is not None:
            self.cnt[dma_sem] = self.cnt.get(dma_sem, 0) + 16
            ev = Ev(dma_sem, self.cnt[dma_sem])
        else:
            self.cnt[eng] = self.cnt.get(eng, 0) + 1
            ev = Ev(eng, self.cnt[eng])
        self.n += 1
        self.evidx[(ev.sem, ev.val)] = self.n
        imp = dict(kn)
        imp.pop(ev.sem, None)
        self.evimp[(ev.sem, ev.val)] = imp
        self.ops.setdefault(eng, []).append((waits, fn, ev, dma_sem is not None))
        for r in reads:
            if ev.val > r.R.get(ev.sem, 0):
                r.R[ev.sem] = ev.val
        for w in writes:
            if ev.val > w.W.get(ev.sem, 0):
                w.W[ev.sem] = ev.val
        return ev


class _Stop(Exception):
    pass


class RR:
    def __init__(self, items):
        self.items = items
        self.i = 0

    def next(self):
        r = self.items[self.i % len(self.items)]
        self.i += 1
        return r


def build_program(segs):
    nseg = len(segs)
    ntile = sum(segs)
    nrows_x = sum(n * TT + 16 for n in segs)
    ntok = ntile * TT

    nc = bass.Bass("TRN2", target_bir_lowering=False)

    def din(name, shape, dt=F32):
        return nc.dram_tensor(name, list(shape), dt, kind="ExternalInput").ap()

    xs = din("xs", [nrows_x, D])
    cT = din("cT", [128, DC * nseg])
    w_ada = din("w_ada", [D, N_MOD * D])
    b_adaT = din("b_adaT", [128, N_MOD * DC])
    wd = {}
    for f in (1, 2):
        wd[f] = (din(f"f{f}w1", [D, FF]), din(f"f{f}w3", [D, FF]), din(f"f{f}w2", [FF, D]))
    w_in = din("w_in", [D, IN_W])
    w_pool = din("w_pool", [4, 256, 256])
    w_out = din("w_out", [D, D])
    smallT = din("smallT", [128, 8 + 24 + 48 + 48])
    invc = din("invc", [ntile, 128, 4 * TT])
    vmask = din("vmask", [128, nseg * 16])
    identd = din("ident", [128, 128])
    y = nc.dram_tensor("y", [ntok, D], F32, kind="ExternalOutput").ap()

    def dscr(name, shape):
        return nc.dram_tensor(name, list(shape), BF16, kind="Internal").ap()

    S13 = {f: dscr(f"s13_{f}", [FC, 128, SLOT]) for f in (1, 2)}
    S2 = {f: dscr(f"s2_{f}", [2 * DC, 128, KH[0] * 128]) for f in (1, 2)}
    Sin = dscr("s_in", [16, 128, SLOT])
    Sout = dscr("s_out", [8, 128, SLOT])

    P = Prog()
    es = ExitStack()
    with es:
        def sb(name, shape, dt=F32):
            return es.enter_context(nc.sbuf_tensor(name, list(shape), dt))

        X = sb("X", [128, DC, FW])
        XM = sb("XM", [128, DC, FW], BF16)
        H = sb("H", [128, FC * FW], BF16)
        RING = sb("RING", [128, NRING, SLOT], BF16)
        NXS = 8
        XST = sb("XST", [128, NXS, 512])
        SCR = sb("SCR", [128, NSCR, FW])
        ACC = sb("ACC", [128, FW])
        ACCSQ = sb("ACCSQ", [128, FW])
        MEAN = sb("MEAN", [128, FW])
        RSTD = sb("RSTD", [128, FW])
        INVC = sb("INVC", [128, 4 * TT])
        MODS = sb("MODS", [128, N_MOD * DC, nseg])
        BM = sb("BM", [128, 2, DC, nseg])
        GS = sb("GS", [128, 2, DC, nseg])
        SMALL = sb("SMALL", [128, 128])
        BADA = sb("BADA", [128, N_MOD * DC])
        VM = sb("VM", [128, nseg * 16])
        IDENT = sb("IDENT", [128, 128])
        ONES = sb("ONES", [128, 128])
        CTF = sb("CTF", [128, DC * nseg])
        SCT = sb("SCT", [128, DC * nseg], BF16)
        WP = sb("WP", [128, 4, 2, 256], BF16)
        CZ = sb("CZ", [128, nseg, PC, 16])
        CU = sb("CU", [128, nseg, PC, 16])
        CB = sb("CB", [128, nseg, PC, 8])

        banks = [es.enter_context(nc.psum_tensor(f"pb{i}", [128, 512], F32)) for i in range(8)]

        sem_names = ["pe", "act", "dve", "pool", "setup"]
        sem_names += [f"ring{i}" for i in range(NRING)]
        sem_names += [f"xst{i}" for i in range(NXS)] + [f"xsw{i}" for i in range(NXS)] + [f"yst{i}" for i in range(16)]
        sem_names += [f"cv{i}" for i in range(8)] + ["ada0", "ada1", "invc", "wp"]
        SEM = {n: es.enter_context(nc.semaphore(n)) for n in sem_names}
        block = es.enter_context(nc.Block())

        rX = [Res() for _ in range(DC)]
        rXM = [Res() for _ in range(DC)]
        rH = [Res() for _ in range(FC)]
        rHEADS = [Res() for _ in range(DC)]
        rPOOLED = [Res() for _ in range(PC)]
        rRING = [Res(RING[:, i, :]) for i in range(NRING)]
        rXST = [Res(XST[:, i, :]) for i in range(NXS)]
        YH = H[:, 0:16 * 1024].bitcast(F32).rearrange("p (k c) -> p k c", c=512)
        rYST = [Res(YH[:, i, :]) for i in range(16)]
        rSCR = RR([Res(SCR[:, i, :]) for i in range(NSCR)])
        bank_res = [Res(banks[i]) for i in range(NBANK)]
        rBANK = RR(bank_res)
        rEBANK = Res(banks[7])
        rACC, rACCSQ, rMEAN, rRSTD, rINVC = Res(), Res(), Res(), Res(), Res()
        rMODS, rBM, rSETUP, rADAB = Res(), Res(), Res(), Res()
        rCZ, rCU, rCB = Res(), Res(), Res()
        HEADS = H[:, 0:DC * TT].rearrange("p (c k) -> p c k", k=TT)
        POOLED = H[:, DC * TT:(DC + PC) * TT].rearrange("p (c k) -> p c k", k=TT)
        Hv = H[:, :].rearrange("p (c k) -> p c k", k=FW)
        ADAB = [H[:, i * 8192:(i + 1) * 8192].rearrange("p (kc j) -> p kc j", j=512) for i in range(2)]
        rADA = [Res(), Res()]

        ring_i = [0]

        def ring_load(src_ap, ncols, src_res):
            i = ring_i[0] % NRING
            ring_i[0] += 1
            r = rRING[i]
            dst = RING[:, i, 0:ncols]
            P.emit("sp", lambda e: e.dma_start(out=dst, in_=src_ap), reads=[src_res], writes=[r],
                   dma_sem=f"ring{i}")
            return r, RING[:, i, :]

        def emit_all():
            def setup_dma(dst, src):
                P.emit("sp", lambda e: e.dma_start(out=dst, in_=src), writes=[rSETUP], dma_sem="setup")

            setup_dma(CTF[:], cT)
            setup_dma(BADA[:], b_adaT)
            setup_dma(SMALL[:], smallT)
            setup_dma(VM[:], vmask)
            setup_dma(IDENT[:], identd)
            P.emit("dve", lambda e: e.memset(ONES[:], 1.0), writes=[rSETUP])
            SP_ = SMALL[:, 0:8]
            WCV = SMALL[:, 8:32].rearrange("p (k c) -> p k c", c=8)
            LNG = SMALL[:, 32:80].rearrange("p (l c) -> p l c", c=DC)
            LNB = SMALL[:, 80:128].rearrange("p (l c) -> p l c", c=DC)

            P.emit("act", lambda e: e.activation(out=SCT[:], in_=CTF[:], func=AF.Silu), reads=[rSETUP], writes=[rSETUP])

            NAP = N_MOD * D // 512
            adabank = bank_res[6]
            for pc in range(NAP):
                bi = pc % 2
                buf = ADAB[bi]
                src = w_ada[:, pc * 512:(pc + 1) * 512].rearrange("(kc p) j -> p kc j", p=128)
                P.emit("pool", (lambda e, buf=buf, src=src: e.dma_start(out=buf, in_=src)), writes=[rADA[bi]],
                       dma_sem=f"ada{bi}")

                def fn(e, buf=buf, pc=pc):
                    ins = None
                    for j in range(4):
                        ch = pc * 4 + j
                        for kc in range(DC):
                            ins = e.matmul(banks[6][:, ch * nseg:(ch + 1) * nseg], buf[:, kc, j * 128:(j + 1) * 128],
                                           SCT[:, kc * nseg:(kc + 1) * nseg], start=(kc == 0), stop=(kc == DC - 1))
                    return ins
                P.emit("pe", fn, reads=[rADA[bi], rSETUP], writes=[adabank])
            NCH = N_MOD * DC
            for s in range(nseg):
                P.emit("dve", (lambda e, s=s: e.tensor_tensor(
                    out=MODS[:, :, s], in0=banks[6][:, 0:NCH * nseg].rearrange("p (c s) -> p c s", s=nseg)[:, :, s],
                    in1=BADA[:], op=ALU.add)), reads=[adabank, rSETUP], writes=[rMODS])
            gfac = (0.5 / ALPHA, 1.0 / ALPHA, 0.5 / ALPHA)
            for l in range(3):
                sc = MODS[:, (3 * l + 1) * DC:(3 * l + 2) * DC, :]
                P.emit("dve", (lambda e, sc=sc: e.tensor_scalar(out=sc, in0=sc, scalar1=1.0, scalar2=None, op0=ALU.add)),
                       reads=[rMODS], writes=[rMODS])
                g = MODS[:, (3 * l + 2) * DC:(3 * l + 3) * DC, :]
                P.emit("dve", (lambda e, g=g, l=l: e.tensor_scalar(out=g, in0=g, scalar1=1.0, scalar2=gfac[l],
                                                                   op0=ALU.add, op1=ALU.mult)),
                       reads=[rMODS], writes=[rMODS])
            for l in range(2):
                for s in range(nseg):
                    scn = MODS[:, (3 * (l + 1) + 1) * DC:(3 * (l + 1) + 2) * DC, s]
                    shn = MODS[:, (3 * (l + 1)) * DC:(3 * (l + 1) + 1) * DC, s]
                    P.emit("dve", (lambda e, l=l, s=s, scn=scn: e.tensor_tensor(out=BM[:, l, :, s], in0=LNB[:, l, :], in1=scn,
                                                                               op=ALU.mult)),
                           reads=[rMODS, rSETUP], writes=[rBM])
                    P.emit("dve", (lambda e, l=l, s=s, shn=shn: e.tensor_tensor(out=BM[:, l, :, s], in0=BM[:, l, :, s], in1=shn,
                                                                               op=ALU.add)),
                           reads=[rMODS, rBM], writes=[rBM])
                    P.emit("dve", (lambda e, l=l, s=s, scn=scn: e.tensor_tensor(out=GS[:, l, :, s], in0=LNG[:, l, :], in1=scn,
                                                                               op=ALU.mult)),
                           reads=[rMODS, rSETUP], writes=[rBM])

            def mod_ap(g, c, s):
                return MODS[:, g * DC + c, s:s + 1]

            stop = 0
            a1lvl = 9
            if stop == 1:
                raise _Stop()
            rWP = Res()
            P.emit("pool", lambda e: e.dma_start(out=WP[:], in_=w_pool.rearrange("g (kc p) j -> p g kc j", p=128)),
                   writes=[rWP], dma_sem="wp")

            cv_i = [0]
            rCVS = [Res() for _ in range(8)]

            def conv_dma(dst, src, piece_res):
                i = cv_i[0] % 8
                cv_i[0] += 1
                ev = P.emit("pool", lambda e: e.dma_start(out=dst, in_=src), writes=[rCVS[i]], dma_sem=f"cv{i}")
                if ev.val > piece_res.W.get(ev.sem, 0):
                    piece_res.W[ev.sem] = ev.val

            def colsrc(w, r0, nk, c0):
                return w[r0:r0 + nk * 128, c0:c0 + 128].rearrange("(kc p) j -> p kc j", p=128)

            def dstv(t, off, nk):
                return t[:, off:off + nk * 128].rearrange("p (kc j) -> p kc j", j=128)

            r13 = {f: [Res() for _ in range(FC)] for f in (1, 2)}
            r2 = {f: [Res() for _ in range(2 * DC)] for f in (1, 2)}
            rin = [Res() for _ in range(16)]
            rout = [Res() for _ in range(8)]
            in_order = []
            for j in range(PC):
                in_order.append((16 + j, 24 + j))
                in_order.append((8 + j, j))
            J_ORDER = list(range(PC - 1, -1, -1))

            def conv_ffn(f):
                w1, w3, w2 = wd[f]
                for fc in range(FC):
                    conv_dma(dstv(S13[f][fc], 0, DC), colsrc(w1, 0, DC, fc * 128), r13[f][fc])
                    conv_dma(dstv(S13[f][fc], 2048, DC), colsrc(w3, 0, DC, fc * 128), r13[f][fc])
                for oc in range(DC):
                    for hf in range(2):
                        conv_dma(dstv(S2[f][2 * oc + hf], 0, KH[hf]), colsrc(w2, hf * KH[0] * 128, KH[hf], oc * 128),
                                 r2[f][2 * oc + hf])
            conv_ffn(1)
            for j in J_ORDER:
                for pc in (2 * j, 2 * j + 1):
                    for e_ in range(2):
                        conv_dma(dstv(Sin[pc], e_ * 2048, DC), colsrc(w_in, 0, DC, in_order[pc][e_] * 128), rin[pc])
            for pc in range(8):
                for e_ in range(2):
                    conv_dma(dstv(Sout[pc], e_ * 2048, DC), colsrc(w_out, 0, DC, (2 * pc + e_) * 128), rout[pc])
            conv_ffn(2)

            if stop == 2:
                raise _Stop()
            def mm_cols(lo, hi):
                r = rBANK.next()
                return r, r.ap[:, 0:hi - lo]

            def ln_accumulate(c, colr, first):
                for (lo, hi) in colr:
                    xa = X[:, c, lo:hi]
                    if first:
                        P.emit("dve", (lambda e, xa=xa, lo=lo, hi=hi: e.tensor_copy(out=ACC[:, lo:hi], in_=xa)),
                               reads=[rX[c]], writes=[rACC])
                        P.emit("act", (lambda e, xa=xa, lo=lo, hi=hi: e.activation(out=ACCSQ[:, lo:hi], in_=xa, func=AF.Square)),
                               reads=[rX[c]], writes=[rACCSQ])
                    else:
                        P.emit("dve", (lambda e, xa=xa, lo=lo, hi=hi: e.tensor_tensor(out=ACC[:, lo:hi], in0=ACC[:, lo:hi], in1=xa,
                                                                                     op=ALU.add)),
                               reads=[rX[c], rACC], writes=[rACC])
                        sq = rSCR.next()
                        n = hi - lo
                        P.emit("act", (lambda e, xa=xa, sq=sq, n=n: e.activation(out=sq.ap[:, 0:n], in_=xa, func=AF.Square)),
                               reads=[rX[c]], writes=[sq])
                        P.emit("dve", (lambda e, sq=sq, lo=lo, hi=hi, n=n: e.tensor_tensor(out=ACCSQ[:, lo:hi], in0=ACCSQ[:, lo:hi],
                                                                                          in1=sq.ap[:, 0:n], op=ALU.add)),
                               reads=[sq, rACCSQ], writes=[rACCSQ])

            pool_ok = [False]

            def layer_norm(l, s, colr, make_xm, after_chunk=None):
                for (lo, hi) in colr:
                    n = hi - lo
                    rs, aps = mm_cols(lo, hi)
                    rq, apq = mm_cols(lo, hi)
                    P.emit("pe", (lambda e, aps=aps, lo=lo, hi=hi: e.matmul(aps, ONES[:], ACC[:, lo:hi], start=True, stop=True)),
                           reads=[rACC, rSETUP], writes=[rs])
                    P.emit("pe", (lambda e, apq=apq, lo=lo, hi=hi: e.matmul(apq, ONES[:], ACCSQ[:, lo:hi], start=True, stop=True)),
                           reads=[rACCSQ, rSETUP], writes=[rq])
                    P.emit("dve", (lambda e, aps=aps, lo=lo, hi=hi: e.tensor_scalar(out=MEAN[:, lo:hi], in0=aps, scalar1=1.0 / D,
                                                                                   scalar2=None, op0=ALU.mult)),
                           reads=[rs], writes=[rMEAN])
                    m2 = rSCR.next()
                    P.emit("dve", (lambda e, m2=m2, lo=lo, hi=hi, n=n: e.tensor_tensor(out=m2.ap[:, 0:n], in0=MEAN[:, lo:hi],
                                                                                      in1=MEAN[:, lo:hi], op=ALU.mult)),
                           reads=[rMEAN], writes=[m2])
                    P.emit("dve", (lambda e, m2=m2, apq=apq, lo=lo, hi=hi, n=n: e.scalar_tensor_tensor(
                        out=RSTD[:, lo:hi], in0=apq, scalar=1.0 / D, in1=m2.ap[:, 0:n], op0=ALU.mult, op1=ALU.subtract)),
                        reads=[rq, m2], writes=[rRSTD])
                    P.emit("dve", (lambda e, lo=lo, hi=hi: e.tensor_scalar(out=RSTD[:, lo:hi], in0=RSTD[:, lo:hi], scalar1=0.0,
                                                                          scalar2=EPS_P, op0=ALU.max, op1=ALU.add)),
                           reads=[rRSTD], writes=[rRSTD])
                    P.emit("act", (lambda e, lo=lo, hi=hi: e.activation(out=RSTD[:, lo:hi], in_=RSTD[:, lo:hi], func=AF.Sqrt)),
                           reads=[rRSTD], writes=[rRSTD])
                    P.emit("dve", (lambda e, lo=lo, hi=hi: e.reciprocal(out=RSTD[:, lo:hi], in_=RSTD[:, lo:hi])),
                           reads=[rRSTD], writes=[rRSTD])
                for c in range(DC):
                    for (lo, hi) in colr:
                        n = hi - lo
                        t = rSCR.next()
                        v = rSCR.next()
                        xa = X[:, c, lo:hi]
                        use_pool = pool_ok[0] and (c % 2 == 1) and n == TT
                        if use_pool:
                            P.emit("pool", (lambda e, t=t, xa=xa, lo=lo, hi=hi, n=n: e.tensor_tensor(
                                out=t.ap[:, 0:n], in0=xa, in1=MEAN[:, lo:hi], op=ALU.subtract)),
                                reads=[rX[c], rMEAN], writes=[t])
                            P.emit("pool", (lambda e, t=t, v=v, lo=lo, hi=hi, n=n: e.tensor_tensor(
                                out=v.ap[:, 0:n], in0=t.ap[:, 0:n], in1=RSTD[:, lo:hi], op=ALU.mult)),
                                reads=[t, rRSTD], writes=[v])
                            P.emit("act", (lambda e, v=v, xa=xa, c=c, n=n: e.activation(
                                out=xa, in_=v.ap[:, 0:n], func=AF.Identity, bias=LNB[:, l, c:c + 1], scale=LNG[:, l, c:c + 1])),
                                reads=[v, rSETUP], writes=[rX[c]])
                            if make_xm:
                                xma = XM[:, c, lo:hi]
                                P.emit("act", (lambda e, v=v, xma=xma, c=c, n=n: e.activation(
                                    out=xma, in_=v.ap[:, 0:n], func=AF.Identity, bias=BM[:, l, c, s:s + 1],
                                    scale=GS[:, l, c, s:s + 1])),
                                    reads=[v, rBM], writes=[rXM[c]])
                            continue
                        P.emit("dve", (lambda e, t=t, xa=xa, lo=lo, hi=hi, n=n: e.tensor_tensor(
                            out=t.ap[:, 0:n], in0=xa, in1=MEAN[:, lo:hi], op=ALU.subtract)),
                            reads=[rX[c], rMEAN], writes=[t])
                        P.emit("dve", (lambda e, t=t, v=v, c=c, lo=lo, hi=hi, n=n: e.scalar_tensor_tensor(
                            out=v.ap[:, 0:n], in0=t.ap[:, 0:n], scalar=LNG[:, l, c:c + 1], in1=RSTD[:, lo:hi],
                            op0=ALU.mult, op1=ALU.mult)),
                            reads=[t, rRSTD, rSETUP], writes=[v])
                        P.emit("act", (lambda e, v=v, xa=xa, c=c, n=n: e.activation(
                            out=xa, in_=v.ap[:, 0:n], func=AF.Identity, bias=LNB[:, l, c:c + 1], scale=1.0)),
                            reads=[v, rSETUP], writes=[rX[c]])
                        if make_xm:
                            xma = XM[:, c, lo:hi]
                            P.emit("act", (lambda e, v=v, xma=xma, c=c, n=n: e.activation(
                                out=xma, in_=v.ap[:, 0:n], func=AF.Identity, bias=BM[:, l, c, s:s + 1],
                                scale=mod_ap(3 * (l + 1) + 1, c, s))),
                                reads=[v, rBM, rMODS], writes=[rXM[c]])
                    if after_chunk is not None:
                        after_chunk(c)

            def ffn(f, s, colr, gidx):
                def up_evac(fc, lo, hi, ra, apa, rb, apb):
                    n = hi - lo
                    sa = rSCR.next()
                    P.emit("act", (lambda e, sa=sa, apa=apa, n=n: e.activation(out=sa.ap[:, 0:n], in_=apa, func=AF.Silu)),
                           reads=[ra], writes=[sa])
                    P.emit("dve", (lambda e, sa=sa, apb=apb, fc=fc, lo=lo, hi=hi, n=n: e.tensor_tensor(
                        out=Hv[:, fc, lo:hi], in0=sa.ap[:, 0:n], in1=apb, op=ALU.mult)),
                        reads=[sa, rb], writes=[rH[fc]])

                def up_fc(fc, slot_r, slot, lo, hi):
                    ra, apa = mm_cols(lo, hi)
                    rb, apb = mm_cols(lo, hi)

                    def fn(e, slot=slot, apa=apa, apb=apb, lo=lo, hi=hi):
                        ins = None
                        for kc in range(DC):
                            ins = e.matmul(apa, slot[:, kc * 128:(kc + 1) * 128], XM[:, kc, lo:hi],
                                           start=(kc == 0), stop=(kc == DC - 1))
                        for kc in range(DC):
                            ins = e.matmul(apb, slot[:, 2048 + kc * 128:2048 + (kc + 1) * 128], XM[:, kc, lo:hi],
                                           start=(kc == 0), stop=(kc == DC - 1))
                        return ins
                    P.emit("pe", fn, reads=[slot_r] + rXM, writes=[ra, rb])
                    up_evac(fc, lo, hi, ra, apa, rb, apb)

                G = 3
                mlo, mhi = colr[0]
                grp = []
                for fc in range(G):
                    slot_r, slot = ring_load(S13[f][fc], SLOT, r13[f][fc])
                    ra, apa = mm_cols(mlo, mhi)
                    rb, apb = mm_cols(mlo, mhi)
                    grp.append((fc, slot_r, slot, ra, apa, rb, apb))
                for kc in range(DC):
                    def fn(e, kc=kc):
                        ins = None
                        for (fc, slot_r, slot, ra, apa, rb, apb) in grp:
                            ins = e.matmul(apa, slot[:, kc * 128:(kc + 1) * 128], XM[:, kc, mlo:mhi],
                                           start=(kc == 0), stop=(kc == DC - 1))
                            ins = e.matmul(apb, slot[:, 2048 + kc * 128:2048 + (kc + 1) * 128], XM[:, kc, mlo:mhi],
                                           start=(kc == 0), stop=(kc == DC - 1))
                        return ins
                    P.emit("pe", fn, reads=[g_[1] for g_ in grp] + [rXM[kc]],
                           writes=[g_[3] for g_ in grp] + [g_[5] for g_ in grp])
                for (fc, slot_r, slot, ra, apa, rb, apb) in grp:
                    up_evac(fc, mlo, mhi, ra, apa, rb, apb)
                    for (lo, hi) in colr[1:]:
                        up_fc(fc, slot_r, slot, lo, hi)
                for fc in range(G, FC):
                    slot_r, slot = ring_load(S13[f][fc], SLOT, r13[f][fc])
                    for (lo, hi) in colr:
                        up_fc(fc, slot_r, slot, lo, hi)
                for oc in range(DC):
                    dests = [mm_cols(lo, hi) for (lo, hi) in colr]
                    for hf in range(2):
                        nk = KH[hf]
                        slot_r, slot = ring_load(S2[f][2 * oc + hf][:, 0:nk * 128], nk * 128, r2[f][2 * oc + hf])
                        for ci, (lo, hi) in enumerate(colr):
                            rd, apd = dests[ci]

                            def fn(e, slot=slot, apd=apd, lo=lo, hi=hi, hf=hf, nk=nk):
                                ins = None
                                for k in range(nk):
                                    kc = hf * KH[0] + k
                                    ins = e.matmul(apd, slot[:, k * 128:(k + 1) * 128], Hv[:, kc, lo:hi],
                                                   start=(kc == 0), stop=(kc == FC - 1))
                                return ins
                            P.emit("pe", fn, reads=[slot_r] + rH[hf * KH[0]:hf * KH[0] + nk], writes=[rd])
                    for ci, (lo, hi) in enumerate(colr):
                        rd, apd = dests[ci]
                        xa = X[:, oc, lo:hi]
                        P.emit("dve", (lambda e, apd=apd, xa=xa, oc=oc: e.scalar_tensor_tensor(
                            out=xa, in0=apd, scalar=mod_ap(gidx, oc, s), in1=xa, op0=ALU.mult, op1=ALU.add)),
                            reads=[rd, rX[oc], rMODS], writes=[rX[oc]])
                    ln_accumulate(oc, colr, oc == 0)

            BW = (8, 8 + TT)
            tile_id = 0
            xrow = 0
            yst_i = 0
            xst_i = 0
            tiles = []
            xr_ = 0
            for s_, ntl_ in enumerate(segs):
                for i_ in range(ntl_):
                    tiles.append((s_, i_, xr_ + 16 + TT * i_))
                xr_ += ntl_ * TT + 16
            xpre = {}
            xst_c = [0]

            def xload(tid, fg, tb):
                key = (tid, fg, tb)
                if key in xpre:
                    return xpre.pop(key)
                k = xst_c[0] % NXS
                xst_c[0] += 1
                st = rXST[k]
                rm = tiles[tid][2]
                src = xs[rm + tb * 128:rm + (tb + 1) * 128, fg * 512:(fg + 1) * 512]
                P.emit("sp", (lambda e, st=st, src=src: e.dma_start(out=st.ap, in_=src)),
                       writes=[st], dma_sem=f"xst{k}")
                return st

            for s, ntl in enumerate(segs):
                for i in range(ntl):
                    first = (i == 0)
                    last = (i == ntl - 1)
                    acols = [(16, FW)] + ([(0, 16)] if first else [])
                    bcols = [BW]
                    row_main = xrow + 16 + TT * i
                    P.emit("sp", (lambda e, tile_id=tile_id: e.dma_start(out=INVC[:], in_=invc[tile_id])),
                           writes=[rINVC], dma_sem="invc")

                    if first:
                        ebank = rEBANK
                    for fg in range(4):
                        cb = [rBANK.next() for _ in range(4)]
                        for tb in range(4):
                            st = xload(tile_id, fg, tb)

                            def fn(e, st=st, cb=cb, tb=tb):
                                ins = None
                                for c4 in range(4):
                                    ins = e.transpose(cb[c4].ap[:, tb * 128:(tb + 1) * 128], st.ap[:, c4 * 128:(c4 + 1) * 128],
                                                      IDENT[:])
                                return ins
                            if a1lvl >= 3:
                                P.emit("pe", fn, reads=[st, rSETUP], writes=cb)
                        if first and a1lvl >= 4:
                            k_ = xst_c[0] % NXS
                            xst_c[0] += 1
                            st = rXST[k_]
                            src = xs[xrow:xrow + 16, fg * 512:(fg + 1) * 512]
                            P.emit("sp", (lambda e, st=st, src=src: e.dma_start(out=st.ap[0:16, :], in_=src)), writes=[st],
                                   dma_sem=f"xst{k_}")

                            def fn(e, st=st, fg=fg, ebank=ebank):
                                ins = None
                                for c4 in range(4):
                                    c = fg * 4 + c4
                                    ins = e.transpose(ebank.ap[:, c * 16:(c + 1) * 16], st.ap[0:16, c4 * 128:(c4 + 1) * 128],
                                                      IDENT[0:16, 0:16])
                                return ins
                            P.emit("pe", fn, reads=[st, rSETUP], writes=[ebank])
                        for c4 in range(4):
                            c = fg * 4 + c4
                            bk = cb[c4]
                            if a1lvl >= 5:
                                P.emit("act", (lambda e, bk=bk, c=c: e.activation(out=X[:, c, 16:FW], in_=bk.ap[:, 0:TT], func=AF.Copy)),
                                       reads=[bk], writes=[rX[c]])
                            if a1lvl < 6:
                                continue
                            P.emit("dve", (lambda e, c=c, s=s: e.tensor_scalar(
                                out=XM[:, c, 16:FW], in0=X[:, c, 16:FW], scalar1=mod_ap(1, c, s), scalar2=mod_ap(0, c, s),
                                op0=ALU.mult, op1=ALU.add)),
                                reads=[rX[c], rMODS], writes=[rXM[c]])
                    if first and a1lvl >= 7:
                        P.emit("act", (lambda e, ebank=ebank: e.activation(
                            out=X[:, :, 0:16], in_=ebank.ap[:, 0:256].rearrange("p (c k) -> p c k", k=16), func=AF.Copy)),
                            reads=[ebank], writes=rX)
                        for c in range(DC):
                            P.emit("dve", (lambda e, c=c, s=s: e.tensor_scalar(
                                out=XM[:, c, 0:16], in0=X[:, c, 0:16], scalar1=mod_ap(1, c, s),
                                scalar2=mod_ap(0, c, s), op0=ALU.mult, op1=ALU.add)),
                                reads=[rX[c], rMODS], writes=[rXM[c]])

                    if stop == 3:
                        raise _Stop()
                    ffn(1, s, acols, 2)
                    layer_norm(0, s, acols, True)

                    if stop == 4:
                        raise _Stop()
                    def in_group(pc, colr, kc_major=False):
                        slot_r, slot = ring_load(Sin[pc], SLOT, rin[pc])
                        outs = []
                        if kc_major:
                            mlo, mhi = colr[0]
                            dm = [mm_cols(mlo, mhi) for _ in range(2)]
                            for kc in range(DC):
                                def fn(e, kc=kc, slot=slot, dm=dm, mlo=mlo, mhi=mhi):
                                    ins = None
                                    for e_ in range(2):
                                        ins = e.matmul(dm[e_][1], slot[:, e_ * 2048 + kc * 128:e_ * 2048 + (kc + 1) * 128],
                                                       XM[:, kc, mlo:mhi], start=(kc == 0), stop=(kc == DC - 1))
                                    return ins
                                P.emit("pe", fn, reads=[slot_r, rXM[kc]], writes=[dm[0][0], dm[1][0]])
                        for e_ in range(2):
                            dd = []
                            for ci, (lo, hi) in enumerate(colr):
                                if kc_major and ci == 0:
                                    dd.append(dm[e_])
                                    continue
                                rd, apd = mm_cols(lo, hi)

                                def fn(e, slot=slot, apd=apd, lo=lo, hi=hi, e_=e_):
                                    ins = None
                                    for kc in range(DC):
                                        ins = e.matmul(apd, slot[:, e_ * 2048 + kc * 128:e_ * 2048 + (kc + 1) * 128],
                                                       XM[:, kc, lo:hi], start=(kc == 0), stop=(kc == DC - 1))
                                    return ins
                                P.emit("pe", fn, reads=[slot_r] + rXM, writes=[rd])
                                dd.append((rd, apd))
                            outs.append(dd)
                        return outs

                    def mask_edges(tl):
                        if first:
                            P.emit("dve", (lambda e, tl=tl, s=s: e.tensor_tensor(out=tl.ap[:, 0:8], in0=tl.ap[:, 0:8],
                                                                                in1=VM[:, s * 16:s * 16 + 8], op=ALU.mult)),
                                   reads=[tl, rSETUP], writes=[tl])
                        if last:
                            P.emit("dve", (lambda e, tl=tl, s=s: e.tensor_tensor(out=tl.ap[:, 520:528], in0=tl.ap[:, 520:528],
                                                                                in1=VM[:, s * 16 + 8:s * 16 + 16], op=ALU.mult)),
                                   reads=[tl, rSETUP], writes=[tl])

                    pending_wp = []
                    for j in J_ORDER:
                        (dC, dV) = in_group(2 * j, acols, kc_major=(j == J_ORDER[0]))
                        while pending_wp:
                            pending_wp.pop(0)()
                        zt = rSCR.next()
                        for ci, (lo, hi) in enumerate(acols):
                            n = hi - lo
                            cs = rSCR.next()
                            rc, apc = dC[ci]
                            rv, apv = dV[ci]
                            P.emit("act", (lambda e, cs=cs, apc=apc, n=n: e.activation(out=cs.ap[:, 0:n], in_=apc, func=AF.Copy)),
                                   reads=[rc], writes=[cs])
                            P.emit("dve", (lambda e, cs=cs, apv=apv, zt=zt, lo=lo, hi=hi, n=n: e.tensor_tensor(
                                out=zt.ap[:, lo:hi], in0=cs.ap[:, 0:n], in1=apv, op=ALU.mult)),
                                reads=[cs, rv], writes=[zt])
                        if not first:
                            P.emit("act", (lambda e, zt=zt, j=j, s=s: e.activation(out=zt.ap[:, 0:16], in_=CZ[:, s, j, :], func=AF.Copy)),
                                   reads=[rCZ], writes=[zt])
                        mask_edges(zt)
                        if not last:
                            P.emit("act", (lambda e, zt=zt, j=j, s=s: e.activation(out=CZ[:, s, j, :], in_=zt.ap[:, 512:528], func=AF.Copy)),
                                   reads=[zt], writes=[rCZ])
                        t1 = rSCR.next()
                        P.emit("dve", (lambda e, zt=zt, t1=t1, j=j: e.tensor_scalar(
                            out=t1.ap[:, 0:TT], in0=zt.ap[:, 7:7 + TT], scalar1=WCV[:, 0, j:j + 1], scalar2=None, op0=ALU.mult)),
                            reads=[zt, rSETUP], writes=[t1])
                        P.emit("dve", (lambda e, zt=zt, t1=t1, j=j: e.scalar_tensor_tensor(
                            out=t1.ap[:, 0:TT], in0=zt.ap[:, 8:8 + TT], scalar=WCV[:, 1, j:j + 1], in1=t1.ap[:, 0:TT],
                            op0=ALU.mult, op1=ALU.add)),
                            reads=[zt, t1, rSETUP], writes=[t1])
                        P.emit("dve", (lambda e, zt=zt, t1=t1, j=j: e.scalar_tensor_tensor(
                            out=t1.ap[:, 0:TT], in0=zt.ap[:, 9:9 + TT], scalar=WCV[:, 2, j:j + 1], in1=t1.ap[:, 0:TT],
                            op0=ALU.mult, op1=ALU.add)),
                            reads=[zt, t1, rSETUP], writes=[t1])
                        (dB, dU) = in_group(2 * j + 1, acols)
                        rb_, apb_ = dB[0]
                        P.emit("dve", (lambda e, t1=t1, apb_=apb_, j=j: e.tensor_tensor(
                            out=HEADS[:, PC + j, 8:TT], in0=t1.ap[:, 8:TT], in1=apb_[:, 0:TT - 8], op=ALU.mult)),
                            reads=[t1, rb_], writes=[rHEADS[PC + j]])
                        if first:
                            rbx, apbx = dB[1]
                            P.emit("dve", (lambda e, t1=t1, apbx=apbx, j=j: e.tensor_tensor(
                                out=HEADS[:, PC + j, 0:8], in0=t1.ap[:, 0:8], in1=apbx[:, 8:16], op=ALU.mult)),
                                reads=[t1, rbx], writes=[rHEADS[PC + j]])
                        else:
                            P.emit("dve", (lambda e, t1=t1, j=j, s=s: e.tensor_tensor(
                                out=HEADS[:, PC + j, 0:8], in0=t1.ap[:, 0:8], in1=CB[:, s, j, :], op=ALU.mult)),
                                reads=[t1, rCB], writes=[rHEADS[PC + j]])
                        if not last:
                            P.emit("dve", (lambda e, apb_=apb_, j=j, s=s: e.tensor_copy(out=CB[:, s, j, :], in_=apb_[:, TT - 8:TT])),
                               reads=[rb_], writes=[rCB])
                        g = j // 2
                        ut = rSCR.next()
                        for ci, (lo, hi) in enumerate(acols):
                            ru, apu = dU[ci]
                            P.emit("act", (lambda e, ut=ut, apu=apu, lo=lo, hi=hi: e.activation(out=ut.ap[:, lo:hi], in_=apu,
                                                                                              func=AF.Copy)),
                                   reads=[ru], writes=[ut])
                        if not first:
                            P.emit("act", (lambda e, ut=ut, j=j, s=s: e.activation(out=ut.ap[:, 0:16], in_=CU[:, s, j, :], func=AF.Copy)),
                                   reads=[rCU], writes=[ut])
                        mask_edges(ut)
                        if not last:
                            P.emit("act", (lambda e, ut=ut, j=j, s=s: e.activation(out=CU[:, s, j, :], in_=ut.ap[:, 512:528], func=AF.Copy)),
                                   reads=[ut], writes=[rCU])
                        cur = ut
                        lo_c, hi_c = 0, FW
                        half = 1
                        first_step = True
                        for _ in range(g + 1):
                            nxt = rSCR.next()
                            if first_step:
                                nlo, nhi = lo_c + 1, hi_c
                                a0, a1 = nlo - 1, nlo
                                first_step = False
                            else:
                                h_ = half // 2
                                nlo, nhi = lo_c + h_, hi_c - h_
                                a0, a1 = nlo - h_, nlo + h_
                            nn = nhi - nlo
                            P.emit("dve", (lambda e, cur=cur, nxt=nxt, a0=a0, a1=a1, nlo=nlo, nn=nn: e.tensor_tensor(
                                out=nxt.ap[:, nlo:nlo + nn], in0=cur.ap[:, a0:a0 + nn], in1=cur.ap[:, a1:a1 + nn], op=ALU.add)),
                                reads=[cur], writes=[nxt])
                            cur = nxt
                            lo_c, hi_c = nlo, nhi
                            half *= 2
                        pm = rSCR.next()
                        P.emit("dve", (lambda e, cur=cur, pm=pm, g=g: e.tensor_tensor(
                            out=pm.ap[:, 0:TT], in0=cur.ap[:, 8:8 + TT], in1=INVC[:, g * TT:(g + 1) * TT], op=ALU.mult)),
                            reads=[cur, rINVC], writes=[pm])
                        P.emit("dve", (lambda e, pm=pm, ut=ut, j=j: e.tensor_tensor(
                            out=POOLED[:, j, :], in0=pm.ap[:, 0:TT], in1=ut.ap[:, 8:8 + TT], op=ALU.subtract)),
                            reads=[pm, ut], writes=[rPOOLED[j]])
                        if j % 2 == 0:
                            def wp_emit(g=g):
                                for o2 in range(2):
                                    rd, apd = mm_cols(0, TT)

                                    def fn(e, apd=apd, g=g, o2=o2):
                                        ins = None
                                        for k2 in range(2):
                                            ins = e.matmul(apd, WP[:, g, k2, o2 * 128:(o2 + 1) * 128], POOLED[:, 2 * g + k2, :],
                                                           start=(k2 == 0), stop=(k2 == 1))
                                        return ins
                                    P.emit("pe", fn, reads=[rWP, rPOOLED[2 * g], rPOOLED[2 * g + 1]], writes=[rd])
                                    hc = 2 * g + o2
                                    P.emit("act", (lambda e, apd=apd, hc=hc: e.activation(out=HEADS[:, hc, :], in_=apd,
                                                                                         func=AF.Identity, scale=SP_[:, hc:hc + 1])),
                                           reads=[rd, rSETUP], writes=[rHEADS[hc]])
                            pending_wp.append(wp_emit)
                    while pending_wp:
                        pending_wp.pop(0)()

                    if stop == 5:
                        raise _Stop()
                    for pc in range(8):
                        slot_r, slot = ring_load(Sout[pc], SLOT, rout[pc])
                        for e_ in range(2):
                            oc = 2 * pc + e_
                            rd, apd = mm_cols(0, TT)

                            def fn(e, slot=slot, apd=apd, e_=e_):
                                ins = None
                                for kc in range(DC):
                                    ins = e.matmul(apd, slot[:, e_ * 2048 + kc * 128:e_ * 2048 + (kc + 1) * 128], HEADS[:, kc, :],
                                                   start=(kc == 0), stop=(kc == DC - 1))
                                return ins
                            P.emit("pe", fn, reads=[slot_r] + rHEADS, writes=[rd])
                            xa = X[:, oc, BW[0]:BW[1]]
                            P.emit("dve", (lambda e, apd=apd, xa=xa, oc=oc, s=s: e.scalar_tensor_tensor(
                                out=xa, in0=apd, scalar=mod_ap(5, oc, s), in1=xa, op0=ALU.mult, op1=ALU.add)),
                                reads=[rd, rX[oc], rMODS], writes=[rX[oc]])
                            ln_accumulate(oc, bcols, oc == 0)
                    layer_norm(1, s, bcols, True)

                    if stop == 6:
                        raise _Stop()
                    if tile_id + 1 < len(tiles):
                        for fg_ in range(2):
                            for tb_ in range(4):
                                st_ = xload(tile_id + 1, fg_, tb_)
                                xpre[(tile_id + 1, fg_, tb_)] = st_
                    ffn(2, s, bcols, 8)
                    def emit_out(fg, tile_id=tile_id):
                        for tb in range(4):
                            bk = rBANK.next()

                            def fn(e, bk=bk, tb=tb, fg=fg):
                                ins = None
                                for c4 in range(4):
                                    c = fg * 4 + c4
                                    ins = e.transpose(bk.ap[:, c4 * 128:(c4 + 1) * 128],
                                                      X[:, c, 8 + tb * 128:8 + (tb + 1) * 128], IDENT[:])
                                return ins
                            P.emit("pe", fn, reads=rX[fg * 4:fg * 4 + 4] + [rSETUP], writes=[bk])
                            ks = tb * 4 + fg
                            st = rYST[ks]
                            sem = f"yst{ks}"
                            ov = [rH[fc_] for fc_ in range(FC) if fc_ * FW < (ks + 1) * 1024 and (fc_ + 1) * FW > ks * 1024]
                            P.emit("act", (lambda e, st=st, bk=bk: e.activation(out=st.ap, in_=bk.ap[:, 0:512], func=AF.Copy)),
                                   reads=[bk], writes=[st])
                            r0 = tile_id * TT + tb * 128
                            dst = y[r0:r0 + 128, fg * 512:(fg + 1) * 512]
                            ev_ = P.emit("pool", (lambda e, st=st, dst=dst: e.dma_start(out=dst, in_=st.ap)), reads=[st],
                                         dma_sem=sem)
                            for r_ in ov:
                                if ev_.val > r_.R.get(ev_.sem, 0):
                                    r_.R[ev_.sem] = ev_.val

                    if stop == 7:
                        layer_norm(2, s, bcols, False)
                        raise _Stop()
                    layer_norm(2, s, bcols, False, after_chunk=(lambda c: emit_out(c // 4) if c % 4 == 3 else None))
                    if not last:
                        P.emit("dve", lambda e: e.tensor_copy(out=X[:, :, 8:16], in_=X[:, :, 520:528]), reads=rX, writes=rX)
                    tile_id += 1
                    pool_ok[0] = False
                xrow += ntl * TT + 16

        try:
            emit_all()
        except _Stop:
            pass

        final_waits = [(f"yst{k}", P.cnt.get(f"yst{k}", 0)) for k in range(16)]

        def replay(e, eng, extra_waits=()):
            for waits, fn, ev, is_dma in P.ops.get(eng, []):
                for (sname, v) in waits:
                    e.wait_ge(SEM[sname], v)
                ins = fn(e)
                ins.then_inc(SEM[ev.sem], 16 if is_dma else 1)
            for (sname, v) in extra_waits:
                if v > 0:
                    e.wait_ge(SEM[sname], v)

        @block.sync
        def _(e):
            replay(e, "sp")

        @block.gpsimd
        def _(e):
            replay(e, "pool", final_waits)

        @block.tensor
        def _(e):
            replay(e, "pe")

        @block.scalar
        def _(e):
            replay(e, "act")

        @block.vector
        def _(e):
            replay(e, "dve")
    return nc


def _feat_major(v):
    v = np.asarray(v, dtype=np.float32)
    return np.ascontiguousarray(v.reshape(-1, 128).T)


def _invcnt(pos, seqlen):
    out = np.empty((4, pos.shape[0]), dtype=np.float32)
    for g, w in enumerate(WINDOWS):
        lo = np.clip(pos - w // 2, 0, seqlen)
        hi = np.clip(pos + w // 2, 0, seqlen)
        out[g] = 1.0 / (hi - lo).astype(np.float32)
    return out


def make_core_inputs(seg_descs, c_list, shared):
    rows = []
    invs = []
    vm = np.zeros((128, len(seg_descs) * 16), dtype=np.float32)
    for si, (xseq, start, ntl) in enumerate(seg_descs):
        L = xseq.shape[0]
        n = ntl * TT
        lo, hi = start - 8, start + n + 8
        seg = np.zeros((n + 16, D), dtype=np.float32)
        a, b = max(lo, 0), min(hi, L)
        seg[a - lo:b - lo] = xseq[a:b]
        rows.append(seg)
        vm[:, si * 16:si * 16 + 8] = 1.0 if lo >= 0 else 0.0
        vm[:, si * 16 + 8:si * 16 + 16] = 1.0 if hi <= L else 0.0
        for i in range(ntl):
            pos = start + i * TT + np.arange(TT)
            iv = _invcnt(pos, L).reshape(1, 4 * TT)
            invs.append(np.broadcast_to(iv, (128, 4 * TT)))
    nseg = len(seg_descs)
    cT = np.empty((128, DC * nseg), dtype=np.float32)
    for si, cv in enumerate(c_list):
        cT[:, si::nseg] = _feat_major(cv)
    m = dict(shared)
    m["xs"] = np.ascontiguousarray(np.concatenate(rows, axis=0))
    m["cT"] = cT
    m["invc"] = np.ascontiguousarray(np.stack(invs, axis=0))
    m["vmask"] = vm
    return m


def make_shared(w_ada, b_ada, ffn1_w1, ffn1_w3, ffn1_w2, w_in, w_pool, s_pool, w_conv, w_out,
                ffn2_w1, ffn2_w3, ffn2_w2, ln_g, ln_b):
    f32 = lambda a: np.ascontiguousarray(np.asarray(a, dtype=np.float32))
    small = np.concatenate([
        _feat_major(s_pool[0]),
        np.concatenate([_feat_major(w_conv[0][k]) for k in range(3)], axis=1),
        np.concatenate([_feat_major(ln_g[0][l]) for l in range(3)], axis=1),
        np.concatenate([_feat_major(ln_b[0][l]) for l in range(3)], axis=1),
    ], axis=1)
    return {
        "w_ada": f32(w_ada[0]), "b_adaT": _feat_major(b_ada[0]),
        "f1w1": f32(ffn1_w1[0]), "f1w3": f32(ffn1_w3[0]), "f1w2": f32(ffn1_w2[0]),
        "f2w1": f32(ffn2_w1[0]), "f2w3": f32(ffn2_w3[0]), "f2w2": f32(ffn2_w2[0]),
        "w_in": f32(w_in[0]), "w_pool": f32(w_pool[0]), "w_out": f32(w_out[0]),
        "smallT": np.ascontiguousarray(small.astype(np.float32)),
        "ident": np.eye(128, dtype=np.float32),
    }


def kernel(x_prompt, x_sample, c_prompt, c_sample, w_ada, b_ada, ffn1_w1, ffn1_w3, ffn1_w2,
           w_in, w_pool, s_pool, w_conv, w_out, ffn2_w1, ffn2_w3, ffn2_w2, ln_g, ln_b):
    x_prompt = np.asarray(x_prompt, dtype=np.float32)
    x_sample = np.asarray(x_sample, dtype=np.float32)
    c_prompt = np.asarray(c_prompt, dtype=np.float32)
    c_sample = np.asarray(c_sample, dtype=np.float32)
    shared = make_shared(*[np.asarray(a) for a in (w_ada, b_ada, ffn1_w1, ffn1_w3, ffn1_w2, w_in, w_pool, s_pool,
                                                   w_conv, w_out, ffn2_w1, ffn2_w3, ffn2_w2, ln_g, ln_b)])
    segs = [4, 8]
    in_maps = []
    for c in range(8):
        q, hh = c // 2, c % 2
        in_maps.append(make_core_inputs(
            [(x_prompt[c], 0, 4), (x_sample[q], hh * 4096, 8)], [c_prompt[c], c_sample[q]], shared))
    nc = build_program(segs)
    res = run_bass_kernel_spmd(nc, in_maps, core_ids=list(range(8)))
    y_prompt = np.empty_like(x_prompt)
    y_sample = np.empty_like(x_sample)
    for c in range(8):
        q, hh = c // 2, c % 2
        yc = res.results[c]["y"]
        y_prompt[c] = yc[0:2048]
        y_sample[q, hh * 4096:(hh + 1) * 4096] = yc[2048:6144]
    return (y_prompt, y_sample)
```

```python
import numpy as np
from contextlib import ExitStack

import concourse.bass as bass
import concourse.mybir as mybir
from concourse.bass_utils import run_bass_kernel_spmd

F32 = mybir.dt.float32
BF16 = mybir.dt.bfloat16
AF = mybir.ActivationFunctionType
ALU = mybir.AluOpType

D = 2048
DC = D // 128
FF = 5504
FC = FF // 128
PW = 1024
PC = PW // 128
IN_W = 4096
N_MOD = 9
ALPHA = 2.0 ** 0.25
LN_EPS = 1e-5
EPS_P = LN_EPS / (ALPHA * ALPHA)
TT = 512
FW = 528
WINDOWS = (2, 4, 8, 16)
KH = (22, 21)
NRING = 5
SLOT = 4096
NSCR = 10
NBANK = 7


class Ev:
    __slots__ = ("sem", "val")

    def __init__(self, sem, val):
        self.sem = sem
        self.val = val


class Res:
    __slots__ = ("W", "R", "ap")

    def __init__(self, ap=None):
        self.W = {}
        self.R = {}
        self.ap = ap


class Prog:
    def __init__(self):
        self.ops = {}
        self.cnt = {}
        self.waited = {}

    def emit(self, eng, fn, reads=(), writes=(), deps=(), dma_sem=None):
        need = {}

        def add(d):
            for s, v in d.items():
                if v > need.get(s, 0):
                    need[s] = v
        for r in reads:
            add(r.W)
        for w in writes:
            add(w.W)
            add(w.R)
        for ev in deps:
            if ev is not None:
                add({ev.sem: ev.val})
        wt = self.waited.setdefault(eng, {})
        waits = []
        for s, v in need.items():
            if eng == "pe" and s == "pe":
                continue
            if v > wt.get(s, 0):
                wt[s] = v
                waits.append((s, v))
        if dma_sem is not None:
            self.cnt[dma_sem] = self.cnt.get(dma_sem, 0) + 16
            ev = Ev(dma_sem, self.cnt[dma_sem])
        else:
            self.cnt[eng] = self.cnt.get(eng, 0) + 1
            ev = Ev(eng, self.cnt[eng])
        self.ops.setdefault(eng, []).append((waits, fn, ev, dma_sem is not None))
        for r in reads:
            if ev.val > r.R.get(ev.sem, 0):
                r.R[ev.sem] = ev.val
        for w in writes:
            if ev.val > w.W.get(ev.sem, 0):
                w.W[ev.sem] = ev.val
        return ev


class _Stop(Exception):
    pass


class RR:
    def __init__(self, items):
        self.items = items
        self.i = 0

    def next(self):
        r = self.items[self.i % len(self.items)]
        self.i += 1
        return r


def build_program(segs):
    nseg = len(segs)
    ntile = sum(segs)
    nrows_x = sum(n * TT + 16 for n in segs)
    ntok = ntile * TT

    nc = bass.Bass("TRN2", target_bir_lowering=False)

    def din(name, shape, dt=F32):
        return nc.dram_tensor(name, list(shape), dt, kind="ExternalInput").ap()

    xs = din("xs", [nrows_x, D])
    cT = din("cT", [128, DC * nseg])
    w_ada = din("w_ada", [D, N_MOD * D])
    b_adaT = din("b_adaT", [128, N_MOD * DC])
    wd = {}
    for f in (1, 2):
        wd[f] = (din(f"f{f}w1", [D, FF]), din(f"f{f}w3", [D, FF]), din(f"f{f}w2", [FF, D]))
    w_in = din("w_in", [D, IN_W])
    w_pool = din("w_pool", [4, 256, 256])
    w_out = din("w_out", [D, D])
    smallT = din("smallT", [128, 8 + 24 + 48 + 48])
    invc = din("invc", [ntile, 128, 4 * TT])
    vmask = din("vmask", [128, nseg * 16])
    identd = din("ident", [128, 128])
    y = nc.dram_tensor("y", [ntok, D], F32, kind="ExternalOutput").ap()

    def dscr(name, shape):
        return nc.dram_tensor(name, list(shape), BF16, kind="Internal").ap()

    S13 = {f: dscr(f"s13_{f}", [FC, 128, SLOT]) for f in (1, 2)}
    S2 = {f: dscr(f"s2_{f}", [2 * DC, 128, KH[0] * 128]) for f in (1, 2)}
    Sin = dscr("s_in", [16, 128, SLOT])
    Sout = dscr("s_out", [8, 128, SLOT])

    P = Prog()
    es = ExitStack()
    with es:
        def sb(name, shape, dt=F32):
            return es.enter_context(nc.sbuf_tensor(name, list(shape), dt))

        X = sb("X", [128, DC, FW])
        XM = sb("XM", [128, DC, FW], BF16)
        H = sb("H", [128, FC * FW], BF16)
        RING = sb("RING", [128, NRING, SLOT], BF16)
        XST = sb("XST", [128, 4, 512])
        YST = sb("YST", [128, 4, 512])
        SCR = sb("SCR", [128, NSCR, FW])
        ACC = sb("ACC", [128, FW])
        ACCSQ = sb("ACCSQ", [128, FW])
        MEAN = sb("MEAN", [128, FW])
        RSTD = sb("RSTD", [128, FW])
        INVC = sb("INVC", [128, 4 * TT])
        MODS = sb("MODS", [128, N_MOD * DC, nseg])
        BM = sb("BM", [128, 2, DC, nseg])
        SMALL = sb("SMALL", [128, 128])
        BADA = sb("BADA", [128, N_MOD * DC])
        VM = sb("VM", [128, nseg * 16])
        IDENT = sb("IDENT", [128, 128])
        ONES = sb("ONES", [128, 128])
        CTF = sb("CTF", [128, DC * nseg])
        SCT = sb("SCT", [128, DC * nseg], BF16)
        WP = sb("WP", [128, 4, 2, 256], BF16)
        CZ = sb("CZ", [128, nseg, PC, 16])
        CU = sb("CU", [128, nseg, PC, 16])
        CB = sb("CB", [128, nseg, PC, 8])

        banks = [es.enter_context(nc.psum_tensor(f"pb{i}", [128, 512], F32)) for i in range(8)]

        sem_names = ["pe", "act", "dve", "pool", "setup"]
        sem_names += [f"ring{i}" for i in range(NRING)]
        sem_names += [f"xst{i}" for i in range(4)] + [f"xsw{i}" for i in range(4)] + [f"yst{i}" for i in range(4)]
        sem_names += [f"cv{i}" for i in range(8)] + ["ada0", "ada1", "invc", "wp"]
        SEM = {n: es.enter_context(nc.semaphore(n)) for n in sem_names}
        block = es.enter_context(nc.Block())

        rX = [Res() for _ in range(DC)]
        rXM = [Res() for _ in range(DC)]
        rH = [Res() for _ in range(FC)]
        rHEADS = [Res() for _ in range(DC)]
        rPOOLED = [Res() for _ in range(PC)]
        rRING = [Res(RING[:, i, :]) for i in range(NRING)]
        rXST = [Res(XST[:, i, :]) for i in range(4)]
        rYST = [Res(YST[:, i, :]) for i in range(4)]
        rSCR = RR([Res(SCR[:, i, :]) for i in range(NSCR)])
        bank_res = [Res(banks[i]) for i in range(NBANK)]
        rBANK = RR(bank_res)
        rEBANK = Res(banks[7])
        rACC, rACCSQ, rMEAN, rRSTD, rINVC = Res(), Res(), Res(), Res(), Res()
        rMODS, rBM, rSETUP, rADAB = Res(), Res(), Res(), Res()
        rCZ, rCU, rCB = Res(), Res(), Res()
        HEADS = H[:, 0:DC * TT].rearrange("p (c k) -> p c k", k=TT)
        POOLED = H[:, DC * TT:(DC + PC) * TT].rearrange("p (c k) -> p c k", k=TT)
        Hv = H[:, :].rearrange("p (c k) -> p c k", k=FW)
        ADAB = [H[:, i * 8192:(i + 1) * 8192].rearrange("p (kc j) -> p kc j", j=512) for i in range(2)]
        rADA = [Res(), Res()]

        ring_i = [0]

        def ring_load(src_ap, ncols, src_res):
            i = ring_i[0] % NRING
            ring_i[0] += 1
            r = rRING[i]
            dst = RING[:, i, 0:ncols]
            P.emit("sp", lambda e: e.dma_start(out=dst, in_=src_ap), reads=[src_res], writes=[r],
                   dma_sem=f"ring{i}")
            return r, RING[:, i, :]

        def emit_all():
            def setup_dma(dst, src):
                P.emit("sp", lambda e: e.dma_start(out=dst, in_=src), writes=[rSETUP], dma_sem="setup")

            setup_dma(CTF[:], cT)
            setup_dma(BADA[:], b_adaT)
            setup_dma(SMALL[:], smallT)
            setup_dma(VM[:], vmask)
            setup_dma(IDENT[:], identd)
            P.emit("dve", lambda e: e.memset(ONES[:], 1.0), writes=[rSETUP])
            SP_ = SMALL[:, 0:8]
            WCV = SMALL[:, 8:32].rearrange("p (k c) -> p k c", c=8)
            LNG = SMALL[:, 32:80].rearrange("p (l c) -> p l c", c=DC)
            LNB = SMALL[:, 80:128].rearrange("p (l c) -> p l c", c=DC)

            P.emit("act", lambda e: e.activation(out=SCT[:], in_=CTF[:], func=AF.Silu), reads=[rSETUP], writes=[rSETUP])

            NAP = N_MOD * D // 512
            adabank = bank_res[6]
            for pc in range(NAP):
                bi = pc % 2
                buf = ADAB[bi]
                src = w_ada[:, pc * 512:(pc + 1) * 512].rearrange("(kc p) j -> p kc j", p=128)
                P.emit("pool", (lambda e, buf=buf, src=src: e.dma_start(out=buf, in_=src)), writes=[rADA[bi]],
                       dma_sem=f"ada{bi}")

                def fn(e, buf=buf, pc=pc):
                    ins = None
                    for j in range(4):
                        ch = pc * 4 + j
                        for kc in range(DC):
                            ins = e.matmul(banks[6][:, ch * nseg:(ch + 1) * nseg], buf[:, kc, j * 128:(j + 1) * 128],
                                           SCT[:, kc * nseg:(kc + 1) * nseg], start=(kc == 0), stop=(kc == DC - 1))
                    return ins
                P.emit("pe", fn, reads=[rADA[bi], rSETUP], writes=[adabank])
            NCH = N_MOD * DC
            for s in range(nseg):
                P.emit("dve", (lambda e, s=s: e.tensor_tensor(
                    out=MODS[:, :, s], in0=banks[6][:, 0:NCH * nseg].rearrange("p (c s) -> p c s", s=nseg)[:, :, s],
                    in1=BADA[:], op=ALU.add)), reads=[adabank, rSETUP], writes=[rMODS])
            gfac = (0.5 / ALPHA, 1.0 / ALPHA, 0.5 / ALPHA)
            for l in range(3):
                sc = MODS[:, (3 * l + 1) * DC:(3 * l + 2) * DC, :]
                P.emit("dve", (lambda e, sc=sc: e.tensor_scalar(out=sc, in0=sc, scalar1=1.0, scalar2=None, op0=ALU.add)),
                       reads=[rMODS], writes=[rMODS])
                g = MODS[:, (3 * l + 2) * DC:(3 * l + 3) * DC, :]
                P.emit("dve", (lambda e, g=g, l=l: e.tensor_scalar(out=g, in0=g, scalar1=1.0, scalar2=gfac[l],
                                                                   op0=ALU.add, op1=ALU.mult)),
                       reads=[rMODS], writes=[rMODS])
            for l in range(2):
                for s in range(nseg):
                    scn = MODS[:, (3 * (l + 1) + 1) * DC:(3 * (l + 1) + 2) * DC, s]
                    shn = MODS[:, (3 * (l + 1)) * DC:(3 * (l + 1) + 1) * DC, s]
                    P.emit("dve", (lambda e, l=l, s=s, scn=scn: e.tensor_tensor(out=BM[:, l, :, s], in0=LNB[:, l, :], in1=scn,
                                                                               op=ALU.mult)),
                           reads=[rMODS, rSETUP], writes=[rBM])
                    P.emit("dve", (lambda e, l=l, s=s, shn=shn: e.tensor_tensor(out=BM[:, l, :, s], in0=BM[:, l, :, s], in1=shn,
                                                                               op=ALU.add)),
                           reads=[rMODS, rBM], writes=[rBM])

            def mod_ap(g, c, s):
                return MODS[:, g * DC + c, s:s + 1]

            stop = 0
            a1lvl = 9
            if stop == 1:
                raise _Stop()
            rWP = Res()
            P.emit("pool", lambda e: e.dma_start(out=WP[:], in_=w_pool.rearrange("g (kc p) j -> p g kc j", p=128)),
                   writes=[rWP], dma_sem="wp")

            cv_i = [0]
            rCVS = [Res() for _ in range(8)]

            def conv_dma(dst, src, piece_res):
                i = cv_i[0] % 8
                cv_i[0] += 1
                ev = P.emit("pool", lambda e: e.dma_start(out=dst, in_=src), writes=[rCVS[i]], dma_sem=f"cv{i}")
                if ev.val > piece_res.W.get(ev.sem, 0):
                    piece_res.W[ev.sem] = ev.val

            def colsrc(w, r0, nk, c0):
                return w[r0:r0 + nk * 128, c0:c0 + 128].rearrange("(kc p) j -> p kc j", p=128)

            def dstv(t, off, nk):
                return t[:, off:off + nk * 128].rearrange("p (kc j) -> p kc j", j=128)

            r13 = {f: [Res() for _ in range(FC)] for f in (1, 2)}
            r2 = {f: [Res() for _ in range(2 * DC)] for f in (1, 2)}
            rin = [Res() for _ in range(16)]
            rout = [Res() for _ in range(8)]
            in_order = []
            for j in range(PC):
                in_order.append((16 + j, 24 + j))
                in_order.append((8 + j, j))
            J_ORDER = list(range(PC - 1, -1, -1))

            def conv_ffn(f):
                w1, w3, w2 = wd[f]
                for fc in range(FC):
                    conv_dma(dstv(S13[f][fc], 0, DC), colsrc(w1, 0, DC, fc * 128), r13[f][fc])
                    conv_dma(dstv(S13[f][fc], 2048, DC), colsrc(w3, 0, DC, fc * 128), r13[f][fc])
                for oc in range(DC):
                    for hf in range(2):
                        conv_dma(dstv(S2[f][2 * oc + hf], 0, KH[hf]), colsrc(w2, hf * KH[0] * 128, KH[hf], oc * 128),
                                 r2[f][2 * oc + hf])
            conv_ffn(1)
            for j in J_ORDER:
                for pc in (2 * j, 2 * j + 1):
                    for e_ in range(2):
                        conv_dma(dstv(Sin[pc], e_ * 2048, DC), colsrc(w_in, 0, DC, in_order[pc][e_] * 128), rin[pc])
            for pc in range(8):
                for e_ in range(2):
                    conv_dma(dstv(Sout[pc], e_ * 2048, DC), colsrc(w_out, 0, DC, (2 * pc + e_) * 128), rout[pc])
            conv_ffn(2)

            if stop == 2:
                raise _Stop()
            def mm_cols(lo, hi):
                r = rBANK.next()
                return r, r.ap[:, 0:hi - lo]

            def ln_accumulate(c, colr, first):
                for (lo, hi) in colr:
                    xa = X[:, c, lo:hi]
                    if first:
                        P.emit("dve", (lambda e, xa=xa, lo=lo, hi=hi: e.tensor_copy(out=ACC[:, lo:hi], in_=xa)),
                               reads=[rX[c]], writes=[rACC])
                        P.emit("act", (lambda e, xa=xa, lo=lo, hi=hi: e.activation(out=ACCSQ[:, lo:hi], in_=xa, func=AF.Square)),
                               reads=[rX[c]], writes=[rACCSQ])
                    else:
                        P.emit("dve", (lambda e, xa=xa, lo=lo, hi=hi: e.tensor_tensor(out=ACC[:, lo:hi], in0=ACC[:, lo:hi], in1=xa,
                                                                                     op=ALU.add)),
                               reads=[rX[c], rACC], writes=[rACC])
                        sq = rSCR.next()
                        n = hi - lo
                        P.emit("act", (lambda e, xa=xa, sq=sq, n=n: e.activation(out=sq.ap[:, 0:n], in_=xa, func=AF.Square)),
                               reads=[rX[c]], writes=[sq])
                        P.emit("dve", (lambda e, sq=sq, lo=lo, hi=hi, n=n: e.tensor_tensor(out=ACCSQ[:, lo:hi], in0=ACCSQ[:, lo:hi],
                                                                                          in1=sq.ap[:, 0:n], op=ALU.add)),
                               reads=[sq, rACCSQ], writes=[rACCSQ])

            def layer_norm(l, s, colr, make_xm):
                for (lo, hi) in colr:
                    n = hi - lo
                    rs, aps = mm_cols(lo, hi)
                    rq, apq = mm_cols(lo, hi)
                    P.emit("pe", (lambda e, aps=aps, lo=lo, hi=hi: e.matmul(aps, ONES[:], ACC[:, lo:hi], start=True, stop=True)),
                           reads=[rACC, rSETUP], writes=[rs])
                    P.emit("pe", (lambda e, apq=apq, lo=lo, hi=hi: e.matmul(apq, ONES[:], ACCSQ[:, lo:hi], start=True, stop=True)),
                           reads=[rACCSQ, rSETUP], writes=[rq])
                    P.emit("dve", (lambda e, aps=aps, lo=lo, hi=hi: e.tensor_scalar(out=MEAN[:, lo:hi], in0=aps, scalar1=1.0 / D,
                                                                                   scalar2=None, op0=ALU.mult)),
                           reads=[rs], writes=[rMEAN])
                    m2 = rSCR.next()
                    P.emit("dve", (lambda e, m2=m2, lo=lo, hi=hi, n=n: e.tensor_tensor(out=m2.ap[:, 0:n], in0=MEAN[:, lo:hi],
                                                                                      in1=MEAN[:, lo:hi], op=ALU.mult)),
                           reads=[rMEAN], writes=[m2])
                    P.emit("dve", (lambda e, m2=m2, apq=apq, lo=lo, hi=hi, n=n: e.scalar_tensor_tensor(
                        out=RSTD[:, lo:hi], in0=apq, scalar=1.0 / D, in1=m2.ap[:, 0:n], op0=ALU.mult, op1=ALU.subtract)),
                        reads=[rq, m2], writes=[rRSTD])
                    P.emit("dve", (lambda e, lo=lo, hi=hi: e.tensor_scalar(out=RSTD[:, lo:hi], in0=RSTD[:, lo:hi], scalar1=0.0,
                                                                          scalar2=EPS_P, op0=ALU.max, op1=ALU.add)),
                           reads=[rRSTD], writes=[rRSTD])
                    P.emit("act", (lambda e, lo=lo, hi=hi: e.activation(out=RSTD[:, lo:hi], in_=RSTD[:, lo:hi], func=AF.Sqrt)),
                           reads=[rRSTD], writes=[rRSTD])
                    P.emit("dve", (lambda e, lo=lo, hi=hi: e.reciprocal(out=RSTD[:, lo:hi], in_=RSTD[:, lo:hi])),
                           reads=[rRSTD], writes=[rRSTD])
                for c in range(DC):
                    for (lo, hi) in colr:
                        n = hi - lo
                        t = rSCR.next()
                        v = rSCR.next()
                        xa = X[:, c, lo:hi]
                        P.emit("dve", (lambda e, t=t, xa=xa, lo=lo, hi=hi, n=n: e.tensor_tensor(
                            out=t.ap[:, 0:n], in0=xa, in1=MEAN[:, lo:hi], op=ALU.subtract)),
                            reads=[rX[c], rMEAN], writes=[t])
                        P.emit("dve", (lambda e, t=t, v=v, c=c, lo=lo, hi=hi, n=n: e.scalar_tensor_tensor(
                            out=v.ap[:, 0:n], in0=t.ap[:, 0:n], scalar=LNG[:, l, c:c + 1], in1=RSTD[:, lo:hi],
                            op0=ALU.mult, op1=ALU.mult)),
                            reads=[t, rRSTD, rSETUP], writes=[v])
                        P.emit("act", (lambda e, v=v, xa=xa, c=c, n=n: e.activation(
                            out=xa, in_=v.ap[:, 0:n], func=AF.Identity, bias=LNB[:, l, c:c + 1], scale=1.0)),
                            reads=[v, rSETUP], writes=[rX[c]])
                        if make_xm:
                            xma = XM[:, c, lo:hi]
                            P.emit("act", (lambda e, v=v, xma=xma, c=c, n=n: e.activation(
                                out=xma, in_=v.ap[:, 0:n], func=AF.Identity, bias=BM[:, l, c, s:s + 1],
                                scale=mod_ap(3 * (l + 1) + 1, c, s))),
                                reads=[v, rBM, rMODS], writes=[rXM[c]])

            def ffn(f, s, colr, gidx):
                def up_evac(fc, lo, hi, ra, apa, rb, apb):
                    n = hi - lo
                    sa = rSCR.next()
                    P.emit("act", (lambda e, sa=sa, apa=apa, n=n: e.activation(out=sa.ap[:, 0:n], in_=apa, func=AF.Silu)),
                           reads=[ra], writes=[sa])
                    P.emit("dve", (lambda e, sa=sa, apb=apb, fc=fc, lo=lo, hi=hi, n=n: e.tensor_tensor(
                        out=Hv[:, fc, lo:hi], in0=sa.ap[:, 0:n], in1=apb, op=ALU.mult)),
                        reads=[sa, rb], writes=[rH[fc]])

                def up_fc(fc, slot_r, slot, lo, hi):
                    ra, apa = mm_cols(lo, hi)
                    rb, apb = mm_cols(lo, hi)

                    def fn(e, slot=slot, apa=apa, apb=apb, lo=lo, hi=hi):
                        ins = None
                        for kc in range(DC):
                            ins = e.matmul(apa, slot[:, kc * 128:(kc + 1) * 128], XM[:, kc, lo:hi],
                                           start=(kc == 0), stop=(kc == DC - 1))
                        for kc in range(DC):
                            ins = e.matmul(apb, slot[:, 2048 + kc * 128:2048 + (kc + 1) * 128], XM[:, kc, lo:hi],
                                           start=(kc == 0), stop=(kc == DC - 1))
                        return ins
                    P.emit("pe", fn, reads=[slot_r] + rXM, writes=[ra, rb])
                    up_evac(fc, lo, hi, ra, apa, rb, apb)

                G = 3
                mlo, mhi = colr[0]
                grp = []
                for fc in range(G):
                    slot_r, slot = ring_load(S13[f][fc], SLOT, r13[f][fc])
                    ra, apa = mm_cols(mlo, mhi)
                    rb, apb = mm_cols(mlo, mhi)
                    grp.append((fc, slot_r, slot, ra, apa, rb, apb))
                for kc in range(DC):
                    def fn(e, kc=kc):
                        ins = None
                        for (fc, slot_r, slot, ra, apa, rb, apb) in grp:
                            ins = e.matmul(apa, slot[:, kc * 128:(kc + 1) * 128], XM[:, kc, mlo:mhi],
                                           start=(kc == 0), stop=(kc == DC - 1))
                            ins = e.matmul(apb, slot[:, 2048 + kc * 128:2048 + (kc + 1) * 128], XM[:, kc, mlo:mhi],
                                           start=(kc == 0), stop=(kc == DC - 1))
                        return ins
                    P.emit("pe", fn, reads=[g_[1] for g_ in grp] + [rXM[kc]],
                           writes=[g_[3] for g_ in grp] + [g_[5] for g_ in grp])
                for (fc, slot_r, slot, ra, apa, rb, apb) in grp:
                    up_evac(fc, mlo, mhi, ra, apa, rb, apb)
                    for (lo, hi) in colr[1:]:
                        up_fc(fc, slot_r, slot, lo, hi)
                for fc in range(G, FC):
                    slot_r, slot = ring_load(S13[f][fc], SLOT, r13[f][fc])
                    for (lo, hi) in colr:
                        up_fc(fc, slot_r, slot, lo, hi)
                for oc in range(DC):
                    dests = [mm_cols(lo, hi) for (lo, hi) in colr]
                    for hf in range(2):
                        nk = KH[hf]
                        slot_r, slot = ring_load(S2[f][2 * oc + hf][:, 0:nk * 128], nk * 128, r2[f][2 * oc + hf])
                        for ci, (lo, hi) in enumerate(colr):
                            rd, apd = dests[ci]

                            def fn(e, slot=slot, apd=apd, lo=lo, hi=hi, hf=hf, nk=nk):
                                ins = None
                                for k in range(nk):
                                    kc = hf * KH[0] + k
                                    ins = e.matmul(apd, slot[:, k * 128:(k + 1) * 128], Hv[:, kc, lo:hi],
                                                   start=(kc == 0), stop=(kc == FC - 1))
                                return ins
                            P.emit("pe", fn, reads=[slot_r] + rH[hf * KH[0]:hf * KH[0] + nk], writes=[rd])
                    for ci, (lo, hi) in enumerate(colr):
                        rd, apd = dests[ci]
                        xa = X[:, oc, lo:hi]
                        P.emit("dve", (lambda e, apd=apd, xa=xa, oc=oc: e.scalar_tensor_tensor(
                            out=xa, in0=apd, scalar=mod_ap(gidx, oc, s), in1=xa, op0=ALU.mult, op1=ALU.add)),
                            reads=[rd, rX[oc], rMODS], writes=[rX[oc]])
                    ln_accumulate(oc, colr, oc == 0)

            BW = (8, 8 + TT)
            tile_id = 0
            xrow = 0
            yst_i = 0
            xst_i = 0
            tiles = []
            xr_ = 0
            for s_, ntl_ in enumerate(segs):
                for i_ in range(ntl_):
                    tiles.append((s_, i_, xr_ + 16 + TT * i_))
                xr_ += ntl_ * TT + 16
            xpre = {}
            xst_c = [0]

            def xload(tid, fg, tb):
                key = (tid, fg, tb)
                if key in xpre:
                    return xpre.pop(key)
                k = xst_c[0] % 4
                xst_c[0] += 1
                st = rXST[k]
                rm = tiles[tid][2]
                src = xs[rm + tb * 128:rm + (tb + 1) * 128, fg * 512:(fg + 1) * 512]
                P.emit("sp" if tid == 0 else "pool", (lambda e, st=st, src=src: e.dma_start(out=st.ap, in_=src)),
                       writes=[st], dma_sem=(f"xst{k}" if tid == 0 else f"xsw{k}"))
                return st

            for s, ntl in enumerate(segs):
                for i in range(ntl):
                    first = (i == 0)
                    last = (i == ntl - 1)
                    acols = [(16, FW)] + ([(0, 16)] if first else [])
                    bcols = [BW]
                    row_main = xrow + 16 + TT * i
                    P.emit("sp", (lambda e, tile_id=tile_id: e.dma_start(out=INVC[:], in_=invc[tile_id])),
                           writes=[rINVC], dma_sem="invc")

                    if first:
                        ebank = rEBANK
                    for fg in range(4):
                        cb = [rBANK.next() for _ in range(4)]
                        for tb in range(4):
                            st = xload(tile_id, fg, tb)

                            def fn(e, st=st, cb=cb, tb=tb):
                                ins = None
                                for c4 in range(4):
                                    ins = e.transpose(cb[c4].ap[:, tb * 128:(tb + 1) * 128], st.ap[:, c4 * 128:(c4 + 1) * 128],
                                                      IDENT[:])
                                return ins
                            if a1lvl >= 3:
                                P.emit("pe", fn, reads=[st, rSETUP], writes=cb)
                        if first and a1lvl >= 4:
                            k_ = xst_c[0] % 4
                            xst_c[0] += 1
                            st = rXST[k_]
                            src = xs[xrow:xrow + 16, fg * 512:(fg + 1) * 512]
                            P.emit("sp" if tile_id == 0 else "pool",
                                   (lambda e, st=st, src=src: e.dma_start(out=st.ap[0:16, :], in_=src)), writes=[st],
                                   dma_sem=(f"xst{k_}" if tile_id == 0 else f"xsw{k_}"))

                            def fn(e, st=st, fg=fg, ebank=ebank):
                                ins = None
                                for c4 in range(4):
                                    c = fg * 4 + c4
                                    ins = e.transpose(ebank.ap[:, c * 16:(c + 1) * 16], st.ap[0:16, c4 * 128:(c4 + 1) * 128],
                                                      IDENT[0:16, 0:16])
                                return ins
                            P.emit("pe", fn, reads=[st, rSETUP], writes=[ebank])
                        for c4 in range(4):
                            c = fg * 4 + c4
                            bk = cb[c4]
                            if a1lvl >= 5:
                                P.emit("act", (lambda e, bk=bk, c=c: e.activation(out=X[:, c, 16:FW], in_=bk.ap[:, 0:TT], func=AF.Copy)),
                                       reads=[bk], writes=[rX[c]])
                            if a1lvl < 6:
                                continue
                            P.emit("dve", (lambda e, c=c, s=s: e.tensor_scalar(
                                out=XM[:, c, 16:FW], in0=X[:, c, 16:FW], scalar1=mod_ap(1, c, s), scalar2=mod_ap(0, c, s),
                                op0=ALU.mult, op1=ALU.add)),
                                reads=[rX[c], rMODS], writes=[rXM[c]])
                    if first and a1lvl >= 7:
                        P.emit("act", (lambda e, ebank=ebank: e.activation(
                            out=X[:, :, 0:16], in_=ebank.ap[:, 0:256].rearrange("p (c k) -> p c k", k=16), func=AF.Copy)),
                            reads=[ebank], writes=rX)
                        for c in range(DC):
                            P.emit("dve", (lambda e, c=c, s=s: e.tensor_scalar(
                                out=XM[:, c, 0:16], in0=X[:, c, 0:16], scalar1=mod_ap(1, c, s),
                                scalar2=mod_ap(0, c, s), op0=ALU.mult, op1=ALU.add)),
                                reads=[rX[c], rMODS], writes=[rXM[c]])

                    if stop == 3:
                        raise _Stop()
                    ffn(1, s, acols, 2)
                    layer_norm(0, s, acols, True)

                    if stop == 4:
                        raise _Stop()
                    def in_group(pc, colr, kc_major=False):
                        slot_r, slot = ring_load(Sin[pc], SLOT, rin[pc])
                        outs = []
                        if kc_major:
                            mlo, mhi = colr[0]
                            dm = [mm_cols(mlo, mhi) for _ in range(2)]
                            for kc in range(DC):
                                def fn(e, kc=kc, slot=slot, dm=dm, mlo=mlo, mhi=mhi):
                                    ins = None
                                    for e_ in range(2):
                                        ins = e.matmul(dm[e_][1], slot[:, e_ * 2048 + kc * 128:e_ * 2048 + (kc + 1) * 128],
                                                       XM[:, kc, mlo:mhi], start=(kc == 0), stop=(kc == DC - 1))
                                    return ins
                                P.emit("pe", fn, reads=[slot_r, rXM[kc]], writes=[dm[0][0], dm[1][0]])
                        for e_ in range(2):
                            dd = []
                            for ci, (lo, hi) in enumerate(colr):
                                if kc_major and ci == 0:
                                    dd.append(dm[e_])
                                    continue
                                rd, apd = mm_cols(lo, hi)

                                def fn(e, slot=slot, apd=apd, lo=lo, hi=hi, e_=e_):
                                    ins = None
                                    for kc in range(DC):
                                        ins = e.matmul(apd, slot[:, e_ * 2048 + kc * 128:e_ * 2048 + (kc + 1) * 128],
                                                       XM[:, kc, lo:hi], start=(kc == 0), stop=(kc == DC - 1))
                                    return ins
                                P.emit("pe", fn, reads=[slot_r] + rXM, writes=[rd])
                                dd.append((rd, apd))
                            outs.append(dd)
                        return outs

                    def mask_edges(tl):
                        if first:
                            P.emit("dve", (lambda e, tl=tl, s=s: e.tensor_tensor(out=tl.ap[:, 0:8], in0=tl.ap[:, 0:8],
                                                                                in1=VM[:, s * 16:s * 16 + 8], op=ALU.mult)),
                                   reads=[tl, rSETUP], writes=[tl])
                        if last:
                            P.emit("dve", (lambda e, tl=tl, s=s: e.tensor_tensor(out=tl.ap[:, 520:528], in0=tl.ap[:, 520:528],
                                                                                in1=VM[:, s * 16 + 8:s * 16 + 16], op=ALU.mult)),
                                   reads=[tl, rSETUP], writes=[tl])

                    pending_wp = []
                    for j in J_ORDER:
                        (dC, dV) = in_group(2 * j, acols, kc_major=(j == J_ORDER[0]))
                        while pending_wp:
                            pending_wp.pop(0)()
                        zt = rSCR.next()
                        for ci, (lo, hi) in enumerate(acols):
                            n = hi - lo
                            cs = rSCR.next()
                            rc, apc = dC[ci]
                            rv, apv = dV[ci]
                            P.emit("act", (lambda e, cs=cs, apc=apc, n=n: e.activation(out=cs.ap[:, 0:n], in_=apc, func=AF.Copy)),
                                   reads=[rc], writes=[cs])
                            P.emit("dve", (lambda e, cs=cs, apv=apv, zt=zt, lo=lo, hi=hi, n=n: e.tensor_tensor(
                                out=zt.ap[:, lo:hi], in0=cs.ap[:, 0:n], in1=apv, op=ALU.mult)),
                                reads=[cs, rv], writes=[zt])
                        if not first:
                            P.emit("act", (lambda e, zt=zt, j=j, s=s: e.activation(out=zt.ap[:, 0:16], in_=CZ[:, s, j, :], func=AF.Copy)),
                                   reads=[rCZ], writes=[zt])
                        mask_edges(zt)
                        if not last:
                            P.emit("act", (lambda e, zt=zt, j=j, s=s: e.activation(out=CZ[:, s, j, :], in_=zt.ap[:, 512:528], func=AF.Copy)),
                                   reads=[zt], writes=[rCZ])
                        t1 = rSCR.next()
                        P.emit("dve", (lambda e, zt=zt, t1=t1, j=j: e.tensor_scalar(
                            out=t1.ap[:, 0:TT], in0=zt.ap[:, 7:7 + TT], scalar1=WCV[:, 0, j:j + 1], scalar2=None, op0=ALU.mult)),
                            reads=[zt, rSETUP], writes=[t1])
                        P.emit("dve", (lambda e, zt=zt, t1=t1, j=j: e.scalar_tensor_tensor(
                            out=t1.ap[:, 0:TT], in0=zt.ap[:, 8:8 + TT], scalar=WCV[:, 1, j:j + 1], in1=t1.ap[:, 0:TT],
                            op0=ALU.mult, op1=ALU.add)),
                            reads=[zt, t1, rSETUP], writes=[t1])
                        P.emit("dve", (lambda e, zt=zt, t1=t1, j=j: e.scalar_tensor_tensor(
                            out=t1.ap[:, 0:TT], in0=zt.ap[:, 9:9 + TT], scalar=WCV[:, 2, j:j + 1], in1=t1.ap[:, 0:TT],
                            op0=ALU.mult, op1=ALU.add)),
                            reads=[zt, t1, rSETUP], writes=[t1])
                        (dB, dU) = in_group(2 * j + 1, acols)
                        rb_, apb_ = dB[0]
                        P.emit("dve", (lambda e, t1=t1, apb_=apb_, j=j: e.tensor_tensor(
                            out=HEADS[:, PC + j, 8:TT], in0=t1.ap[:, 8:TT], in1=apb_[:, 0:TT - 8], op=ALU.mult)),
                            reads=[t1, rb_], writes=[rHEADS[PC + j]])
                        if first:
                            rbx, apbx = dB[1]
                            P.emit("dve", (lambda e, t1=t1, apbx=apbx, j=j: e.tensor_tensor(
                                out=HEADS[:, PC + j, 0:8], in0=t1.ap[:, 0:8], in1=apbx[:, 8:16], op=ALU.mult)),
                                reads=[t1, rbx], writes=[rHEADS[PC + j]])
                        else:
                            P.emit("dve", (lambda e, t1=t1, j=j, s=s: e.tensor_tensor(
                                out=HEADS[:, PC + j, 0:8], in0=t1.ap[:, 0:8], in1=CB[:, s, j, :], op=ALU.mult)),
                                reads=[t1, rCB], writes=[rHEADS[PC + j]])
                        if not last:
                            P.emit("dve", (lambda e, apb_=apb_, j=j, s=s: e.tensor_copy(out=CB[:, s, j, :], in_=apb_[:, TT - 8:TT])),
                               reads=[rb_], writes=[rCB])
                        g = j // 2
                        ut = rSCR.next()
                        for ci, (lo, hi) in enumerate(acols):
                            ru, apu = dU[ci]
                            P.emit("act", (lambda e, ut=ut, apu=apu, lo=lo, hi=hi: e.activation(out=ut.ap[:, lo:hi], in_=apu,
                                                                                              func=AF.Copy)),
                                   reads=[ru], writes=[ut])
                        if not first:
                            P.emit("act", (lambda e, ut=ut, j=j, s=s: e.activation(out=ut.ap[:, 0:16], in_=CU[:, s, j, :], func=AF.Copy)),
                                   reads=[rCU], writes=[ut])
                        mask_edges(ut)
                        if not last:
                            P.emit("act", (lambda e, ut=ut, j=j, s=s: e.activation(out=CU[:, s, j, :], in_=ut.ap[:, 512:528], func=AF.Copy)),
                                   reads=[ut], writes=[rCU])
                        cur = ut
                        lo_c, hi_c = 0, FW
                        half = 1
                        first_step = True
                        for _ in range(g + 1):
                            nxt = rSCR.next()
                            if first_step:
                                nlo, nhi = lo_c + 1, hi_c
                                a0, a1 = nlo - 1, nlo
                                first_step = False
                            else:
                                h_ = half // 2
                                nlo, nhi = lo_c + h_, hi_c - h_
                                a0, a1 = nlo - h_, nlo + h_
                            nn = nhi - nlo
                            P.emit("dve", (lambda e, cur=cur, nxt=nxt, a0=a0, a1=a1, nlo=nlo, nn=nn: e.tensor_tensor(
                                out=nxt.ap[:, nlo:nlo + nn], in0=cur.ap[:, a0:a0 + nn], in1=cur.ap[:, a1:a1 + nn], op=ALU.add)),
                                reads=[cur], writes=[nxt])
                            cur = nxt
                            lo_c, hi_c = nlo, nhi
                            half *= 2
                        pm = rSCR.next()
                        P.emit("dve", (lambda e, cur=cur, pm=pm, g=g: e.tensor_tensor(
                            out=pm.ap[:, 0:TT], in0=cur.ap[:, 8:8 + TT], in1=INVC[:, g * TT:(g + 1) * TT], op=ALU.mult)),
                            reads=[cur, rINVC], writes=[pm])
                        P.emit("dve", (lambda e, pm=pm, ut=ut, j=j: e.tensor_tensor(
                            out=POOLED[:, j, :], in0=pm.ap[:, 0:TT], in1=ut.ap[:, 8:8 + TT], op=ALU.subtract)),
                            reads=[pm, ut], writes=[rPOOLED[j]])
                        if j % 2 == 0:
                            def wp_emit(g=g):
                                for o2 in range(2):
                                    rd, apd = mm_cols(0, TT)

                                    def fn(e, apd=apd, g=g, o2=o2):
                                        ins = None
                                        for k2 in range(2):
                                            ins = e.matmul(apd, WP[:, g, k2, o2 * 128:(o2 + 1) * 128], POOLED[:, 2 * g + k2, :],
                                                           start=(k2 == 0), stop=(k2 == 1))
                                        return ins
                                    P.emit("pe", fn, reads=[rWP, rPOOLED[2 * g], rPOOLED[2 * g + 1]], writes=[rd])
                                    hc = 2 * g + o2
                                    P.emit("act", (lambda e, apd=apd, hc=hc: e.activation(out=HEADS[:, hc, :], in_=apd,
                                                                                         func=AF.Identity, scale=SP_[:, hc:hc + 1])),
                                           reads=[rd, rSETUP], writes=[rHEADS[hc]])
                            pending_wp.append(wp_emit)
                    while pending_wp:
                        pending_wp.pop(0)()

                    if stop == 5:
                        raise _Stop()
                    for pc in range(8):
                        slot_r, slot = ring_load(Sout[pc], SLOT, rout[pc])
                        for e_ in range(2):
                            oc = 2 * pc + e_
                            rd, apd = mm_cols(0, TT)

                            def fn(e, slot=slot, apd=apd, e_=e_):
                                ins = None
                                for kc in range(DC):
                                    ins = e.matmul(apd, slot[:, e_ * 2048 + kc * 128:e_ * 2048 + (kc + 1) * 128], HEADS[:, kc, :],
                                                   start=(kc == 0), stop=(kc == DC - 1))
                                return ins
                            P.emit("pe", fn, reads=[slot_r] + rHEADS, writes=[rd])
                            xa = X[:, oc, BW[0]:BW[1]]
                            P.emit("dve", (lambda e, apd=apd, xa=xa, oc=oc, s=s: e.scalar_tensor_tensor(
                                out=xa, in0=apd, scalar=mod_ap(5, oc, s), in1=xa, op0=ALU.mult, op1=ALU.add)),
                                reads=[rd, rX[oc], rMODS], writes=[rX[oc]])
                            ln_accumulate(oc, bcols, oc == 0)
                    layer_norm(1, s, bcols, True)

                    if stop == 6:
                        raise _Stop()
                    if tile_id + 1 < len(tiles):
                        for tb_ in range(4):
                            st_ = xload(tile_id + 1, 0, tb_)
                            xpre[(tile_id + 1, 0, tb_)] = st_
                    ffn(2, s, bcols, 8)
                    layer_norm(2, s, bcols, False)

                    if stop == 7:
                        raise _Stop()
                    for tb in range(4):
                        for fg in range(4):
                            bk = rBANK.next()

                            def fn(e, bk=bk, tb=tb, fg=fg):
                                ins = None
                                for c4 in range(4):
                                    c = fg * 4 + c4
                                    ins = e.transpose(bk.ap[:, c4 * 128:(c4 + 1) * 128],
                                                      X[:, c, 8 + tb * 128:8 + (tb + 1) * 128], IDENT[:])
                                return ins
                            P.emit("pe", fn, reads=rX[fg * 4:fg * 4 + 4] + [rSETUP], writes=[bk])
                            st = rYST[yst_i % 4]
                            sem = f"yst{yst_i % 4}"
                            yst_i += 1
                            if (tb * 4 + fg) % 2 == 0:
                                P.emit("act", (lambda e, st=st, bk=bk: e.activation(out=st.ap, in_=bk.ap[:, 0:512], func=AF.Copy)),
                                       reads=[bk], writes=[st])
                            else:
                                P.emit("dve", (lambda e, st=st, bk=bk: e.tensor_copy(out=st.ap, in_=bk.ap[:, 0:512])),
                                       reads=[bk], writes=[st])
                            r0 = tile_id * TT + tb * 128
                            dst = y[r0:r0 + 128, fg * 512:(fg + 1) * 512]
                            P.emit("pool", (lambda e, st=st, dst=dst: e.dma_start(out=dst, in_=st.ap)), reads=[st], dma_sem=sem)
                    if not last:
                        P.emit("dve", lambda e: e.tensor_copy(out=X[:, :, 8:16], in_=X[:, :, 520:528]), reads=rX, writes=rX)
                    tile_id += 1
                xrow += ntl * TT + 16

        try:
            emit_all()
        except _Stop:
            pass

        final_waits = [(f"yst{k}", P.cnt.get(f"yst{k}", 0)) for k in range(4)]

        def replay(e, eng, extra_waits=()):
            for waits, fn, ev, is_dma in P.ops.get(eng, []):
                for (sname, v) in waits:
                    e.wait_ge(SEM[sname], v)
                ins = fn(e)
                ins.then_inc(SEM[ev.sem], 16 if is_dma else 1)
            for (sname, v) in extra_waits:
                if v > 0:
                    e.wait_ge(SEM[sname], v)

        @block.sync
        def _(e):
            replay(e, "sp")

        @block.gpsimd
        def _(e):
            replay(e, "pool", final_waits)

        @block.tensor
        def _(e):
            replay(e, "pe")

        @block.scalar
        def _(e):
            replay(e, "act")

        @block.vector
        def _(e):
            replay(e, "dve")
    return nc


def _feat_major(v):
    v = np.asarray(v, dtype=np.float32)
    return np.ascontiguousarray(v.reshape(-1, 128).T)


def _invcnt(pos, seqlen):
    out = np.empty((4, pos.shape[0]), dtype=np.float32)
    for g, w in enumerate(WINDOWS):
        lo = np.clip(pos - w // 2, 0, seqlen)
        hi = np.clip(pos + w // 2, 0, seqlen)
        out[g] = 1.0 / (hi - lo).astype(np.float32)
    return out


def make_core_inputs(seg_descs, c_list, shared):
    rows = []
    invs = []
    vm = np.zeros((128, len(seg_descs) * 16), dtype=np.float32)
    for si, (xseq, start, ntl) in enumerate(seg_descs):
        L = xseq.shape[0]
        n = ntl * TT
        lo, hi = start - 8, start + n + 8
        seg = np.zeros((n + 16, D), dtype=np.float32)
        a, b = max(lo, 0), min(hi, L)
        seg[a - lo:b - lo] = xseq[a:b]
        rows.append(seg)
        vm[:, si * 16:si * 16 + 8] = 1.0 if lo >= 0 else 0.0
        vm[:, si * 16 + 8:si * 16 + 16] = 1.0 if hi <= L else 0.0
        for i in range(ntl):
            pos = start + i * TT + np.arange(TT)
            iv = _invcnt(pos, L).reshape(1, 4 * TT)
            invs.append(np.broadcast_to(iv, (128, 4 * TT)))
    nseg = len(seg_descs)
    cT = np.empty((128, DC * nseg), dtype=np.float32)
    for si, cv in enumerate(c_list):
        cT[:, si::nseg] = _feat_major(cv)
    m = dict(shared)
    m["xs"] = np.ascontiguousarray(np.concatenate(rows, axis=0))
    m["cT"] = cT
    m["invc"] = np.ascontiguousarray(np.stack(invs, axis=0))
    m["vmask"] = vm
    return m


def make_shared(w_ada, b_ada, ffn1_w1, ffn1_w3, ffn1_w2, w_in, w_pool, s_pool, w_conv, w_out,
                ffn2_w1, ffn2_w3, ffn2_w2, ln_g, ln_b):
    f32 = lambda a: np.ascontiguousarray(np.asarray(a, dtype=np.float32))
    small = np.concatenate([
        _feat_major(s_pool[0]),
        np.concatenate([_feat_major(w_conv[0][k]) for k in range(3)], axis=1),
        np.concatenate([_feat_major(ln_g[0][l]) for l in range(3)], axis=1),
        np.concatenate([_feat_major(ln_b[0][l]) for l in range(3)], axis=1),
    ], axis=1)
    return {
        "w_ada": f32(w_ada[0]), "b_adaT": _feat_major(b_ada[0]),
        "f1w1": f32(ffn1_w1[0]), "f1w3": f32(ffn1_w3[0]), "f1w2": f32(ffn1_w2[0]),
        "f2w1": f32(ffn2_w1[0]), "f2w3": f32(ffn2_w3[0]), "f2w2": f32(ffn2_w2[0]),
        "w_in": f32(w_in[0]), "w_pool": f32(w_pool[0]), "w_out": f32(w_out[0]),
        "smallT": np.ascontiguousarray(small.astype(np.float32)),
        "ident": np.eye(128, dtype=np.float32),
    }


def kernel(x_prompt, x_sample, c_prompt, c_sample, w_ada, b_ada, ffn1_w1, ffn1_w3, ffn1_w2,
           w_in, w_pool, s_pool, w_conv, w_out, ffn2_w1, ffn2_w3, ffn2_w2, ln_g, ln_b):
    x_prompt = np.asarray(x_prompt, dtype=np.float32)
    x_sample = np.asarray(x_sample, dtype=np.float32)
    c_prompt = np.asarray(c_prompt, dtype=np.float32)
    c_sample = np.asarray(c_sample, dtype=np.float32)
    shared = make_shared(*[np.asarray(a) for a in (w_ada, b_ada, ffn1_w1, ffn1_w3, ffn1_w2, w_in, w_pool, s_pool,
                                                   w_conv, w_out, ffn2_w1, ffn2_w3, ffn2_w2, ln_g, ln_b)])
    segs = [4, 8]
    in_maps = []
    for c in range(8):
        q, hh = c // 2, c % 2
        in_maps.append(make_core_inputs(
            [(x_prompt[c], 0, 4), (x_sample[q], hh * 4096, 8)], [c_prompt[c], c_sample[q]], shared))
    nc = build_program(segs)
    res = run_bass_kernel_spmd(nc, in_maps, core_ids=list(range(8)))
    y_prompt = np.empty_like(x_prompt)
    y_sample = np.empty_like(x_sample)
    for c in range(8):
        q, hh = c // 2, c % 2
        yc = res.results[c]["y"]
        y_prompt[c] = yc[0:2048]
        y_sample[q, hh * 4096:(hh + 1) * 4096] = yc[2048:6144]
    return (y_prompt, y_sample)
```

```python
import os
import numpy as np
from contextlib import ExitStack

import concourse.bass as bass
import concourse.mybir as mybir
from concourse.bass_utils import run_bass_kernel_spmd

F32 = mybir.dt.float32
BF16 = mybir.dt.bfloat16
AF = mybir.ActivationFunctionType
ALU = mybir.AluOpType

D = 2048
DC = D // 128
FF = 5504
FC = FF // 128
PW = 1024
PC = PW // 128
IN_W = 4096
N_MOD = 9
ALPHA = 2.0 ** 0.25
LN_EPS = 1e-5
EPS_P = LN_EPS / (ALPHA * ALPHA)
TT = 512
FW = 528
WINDOWS = (2, 4, 8, 16)
KH = (22, 21)
NRING = 5
SLOT = 4096
NSCR = 10
NBANK = 7


class Ev:
    __slots__ = ("sem", "val")

    def __init__(self, sem, val):
        self.sem = sem
        self.val = val


class Res:
    __slots__ = ("W", "R", "ap")

    def __init__(self, ap=None):
        self.W = {}
        self.R = {}
        self.ap = ap


class Prog:
    def __init__(self):
        self.ops = {}
        self.cnt = {}
        self.waited = {}

    def emit(self, eng, fn, reads=(), writes=(), deps=(), dma_sem=None):
        need = {}

        def add(d):
            for s, v in d.items():
                if v > need.get(s, 0):
                    need[s] = v
        for r in reads:
            add(r.W)
        for w in writes:
            add(w.W)
            add(w.R)
        for ev in deps:
            if ev is not None:
                add({ev.sem: ev.val})
        wt = self.waited.setdefault(eng, {})
        waits = []
        for s, v in need.items():
            if eng == "pe" and s == "pe":
                continue
            if v > wt.get(s, 0):
                wt[s] = v
                waits.append((s, v))
        if dma_sem is not None:
            self.cnt[dma_sem] = self.cnt.get(dma_sem, 0) + 16
            ev = Ev(dma_sem, self.cnt[dma_sem])
        else:
            self.cnt[eng] = self.cnt.get(eng, 0) + 1
            ev = Ev(eng, self.cnt[eng])
        self.ops.setdefault(eng, []).append((waits, fn, ev, dma_sem is not None))
        for r in reads:
            if ev.val > r.R.get(ev.sem, 0):
                r.R[ev.sem] = ev.val
        for w in writes:
            if ev.val > w.W.get(ev.sem, 0):
                w.W[ev.sem] = ev.val
        return ev


class _Stop(Exception):
    pass


class RR:
    def __init__(self, items):
        self.items = items
        self.i = 0

    def next(self):
        r = self.items[self.i % len(self.items)]
        self.i += 1
        return r


def build_program(segs):
    nseg = len(segs)
    ntile = sum(segs)
    nrows_x = sum(n * TT + 16 for n in segs)
    ntok = ntile * TT

    nc = bass.Bass("TRN2", target_bir_lowering=False)

    def din(name, shape, dt=F32):
        return nc.dram_tensor(name, list(shape), dt, kind="ExternalInput").ap()

    xs = din("xs", [nrows_x, D])
    cT = din("cT", [128, DC * nseg])
    w_ada = din("w_ada", [D, N_MOD * D])
    b_adaT = din("b_adaT", [128, N_MOD * DC])
    wd = {}
    for f in (1, 2):
        wd[f] = (din(f"f{f}w1", [D, FF]), din(f"f{f}w3", [D, FF]), din(f"f{f}w2", [FF, D]))
    w_in = din("w_in", [D, IN_W])
    w_pool = din("w_pool", [4, 256, 256])
    w_out = din("w_out", [D, D])
    smallT = din("smallT", [128, 8 + 24 + 48 + 48])
    invc = din("invc", [ntile, 128, 4 * TT])
    vmask = din("vmask", [128, nseg * 16])
    identd = din("ident", [128, 128])
    y = nc.dram_tensor("y", [ntok, D], F32, kind="ExternalOutput").ap()

    def dscr(name, shape):
        return nc.dram_tensor(name, list(shape), BF16, kind="Internal").ap()

    S13 = {f: dscr(f"s13_{f}", [FC, 128, SLOT]) for f in (1, 2)}
    S2 = {f: dscr(f"s2_{f}", [2 * DC, 128, KH[0] * 128]) for f in (1, 2)}
    Sin = dscr("s_in", [16, 128, SLOT])
    Sout = dscr("s_out", [8, 128, SLOT])

    P = Prog()
    es = ExitStack()
    with es:
        def sb(name, shape, dt=F32):
            return es.enter_context(nc.sbuf_tensor(name, list(shape), dt))

        X = sb("X", [128, DC, FW])
        XM = sb("XM", [128, DC, FW], BF16)
        H = sb("H", [128, FC * FW], BF16)
        RING = sb("RING", [128, NRING, SLOT], BF16)
        XST = sb("XST", [128, 4, 512])
        YST = sb("YST", [128, 4, 512])
        SCR = sb("SCR", [128, NSCR, FW])
        ACC = sb("ACC", [128, FW])
        ACCSQ = sb("ACCSQ", [128, FW])
        MEAN = sb("MEAN", [128, FW])
        RSTD = sb("RSTD", [128, FW])
        INVC = sb("INVC", [128, 4 * TT])
        MODS = sb("MODS", [128, N_MOD * DC, nseg])
        BM = sb("BM", [128, 2, DC, nseg])
        SMALL = sb("SMALL", [128, 128])
        BADA = sb("BADA", [128, N_MOD * DC])
        VM = sb("VM", [128, nseg * 16])
        IDENT = sb("IDENT", [128, 128])
        ONES = sb("ONES", [128, 128])
        CTF = sb("CTF", [128, DC * nseg])
        SCT = sb("SCT", [128, DC * nseg], BF16)
        WP = sb("WP", [128, 4, 2, 256], BF16)
        CZ = sb("CZ", [128, nseg, PC, 16])
        CU = sb("CU", [128, nseg, PC, 16])
        CB = sb("CB", [128, nseg, PC, 8])

        banks = [es.enter_context(nc.psum_tensor(f"pb{i}", [128, 512], F32)) for i in range(8)]

        sem_names = ["pe", "act", "dve", "pool", "setup"]
        sem_names += [f"ring{i}" for i in range(NRING)]
        sem_names += [f"xst{i}" for i in range(4)] + [f"xsw{i}" for i in range(4)] + [f"yst{i}" for i in range(4)]
        sem_names += [f"cv{i}" for i in range(8)] + ["ada0", "ada1", "invc", "wp"]
        SEM = {n: es.enter_context(nc.semaphore(n)) for n in sem_names}
        block = es.enter_context(nc.Block())

        rX = [Res() for _ in range(DC)]
        rXM = [Res() for _ in range(DC)]
        rH = [Res() for _ in range(FC)]
        rHEADS = [Res() for _ in range(DC)]
        rPOOLED = [Res() for _ in range(PC)]
        rRING = [Res(RING[:, i, :]) for i in range(NRING)]
        rXST = [Res(XST[:, i, :]) for i in range(4)]
        rYST = [Res(YST[:, i, :]) for i in range(4)]
        rSCR = RR([Res(SCR[:, i, :]) for i in range(NSCR)])
        bank_res = [Res(banks[i]) for i in range(NBANK)]
        rBANK = RR(bank_res)
        rEBANK = Res(banks[7])
        rACC, rACCSQ, rMEAN, rRSTD, rINVC = Res(), Res(), Res(), Res(), Res()
        rMODS, rBM, rSETUP, rADAB = Res(), Res(), Res(), Res()
        rCZ, rCU, rCB = Res(), Res(), Res()
        HEADS = H[:, 0:DC * TT].rearrange("p (c k) -> p c k", k=TT)
        POOLED = H[:, DC * TT:(DC + PC) * TT].rearrange("p (c k) -> p c k", k=TT)
        Hv = H[:, :].rearrange("p (c k) -> p c k", k=FW)
        ADAB = [H[:, i * 8192:(i + 1) * 8192].rearrange("p (kc j) -> p kc j", j=512) for i in range(2)]
        rADA = [Res(), Res()]

        ring_i = [0]

        def ring_load(src_ap, ncols, src_res):
            i = ring_i[0] % NRING
            ring_i[0] += 1
            r = rRING[i]
            dst = RING[:, i, 0:ncols]
            P.emit("sp", lambda e: e.dma_start(out=dst, in_=src_ap), reads=[src_res], writes=[r],
                   dma_sem=f"ring{i}")
            return r, RING[:, i, :]

        def emit_all():
            def setup_dma(dst, src):
                P.emit("sp", lambda e: e.dma_start(out=dst, in_=src), writes=[rSETUP], dma_sem="setup")

            setup_dma(CTF[:], cT)
            setup_dma(BADA[:], b_adaT)
            setup_dma(SMALL[:], smallT)
            setup_dma(VM[:], vmask)
            setup_dma(IDENT[:], identd)
            P.emit("dve", lambda e: e.memset(ONES[:], 1.0), writes=[rSETUP])
            SP_ = SMALL[:, 0:8]
            WCV = SMALL[:, 8:32].rearrange("p (k c) -> p k c", c=8)
            LNG = SMALL[:, 32:80].rearrange("p (l c) -> p l c", c=DC)
            LNB = SMALL[:, 80:128].rearrange("p (l c) -> p l c", c=DC)

            P.emit("act", lambda e: e.activation(out=SCT[:], in_=CTF[:], func=AF.Silu), reads=[rSETUP], writes=[rSETUP])

            NAP = N_MOD * D // 512
            adabank = bank_res[6]
            for pc in range(NAP):
                bi = pc % 2
                buf = ADAB[bi]
                src = w_ada[:, pc * 512:(pc + 1) * 512].rearrange("(kc p) j -> p kc j", p=128)
                P.emit("pool", (lambda e, buf=buf, src=src: e.dma_start(out=buf, in_=src)), writes=[rADA[bi]],
                       dma_sem=f"ada{bi}")

                def fn(e, buf=buf, pc=pc):
                    ins = None
                    for j in range(4):
                        ch = pc * 4 + j
                        for kc in range(DC):
                            ins = e.matmul(banks[6][:, ch * nseg:(ch + 1) * nseg], buf[:, kc, j * 128:(j + 1) * 128],
                                           SCT[:, kc * nseg:(kc + 1) * nseg], start=(kc == 0), stop=(kc == DC - 1))
                    return ins
                P.emit("pe", fn, reads=[rADA[bi], rSETUP], writes=[adabank])
            NCH = N_MOD * DC
            for s in range(nseg):
                P.emit("dve", (lambda e, s=s: e.tensor_tensor(
                    out=MODS[:, :, s], in0=banks[6][:, 0:NCH * nseg].rearrange("p (c s) -> p c s", s=nseg)[:, :, s],
                    in1=BADA[:], op=ALU.add)), reads=[adabank, rSETUP], writes=[rMODS])
            gfac = (0.5 / ALPHA, 1.0 / ALPHA, 0.5 / ALPHA)
            for l in range(3):
                sc = MODS[:, (3 * l + 1) * DC:(3 * l + 2) * DC, :]
                P.emit("dve", (lambda e, sc=sc: e.tensor_scalar(out=sc, in0=sc, scalar1=1.0, scalar2=None, op0=ALU.add)),
                       reads=[rMODS], writes=[rMODS])
                g = MODS[:, (3 * l + 2) * DC:(3 * l + 3) * DC, :]
                P.emit("dve", (lambda e, g=g, l=l: e.tensor_scalar(out=g, in0=g, scalar1=1.0, scalar2=gfac[l],
                                                                   op0=ALU.add, op1=ALU.mult)),
                       reads=[rMODS], writes=[rMODS])
            for l in range(2):
                for s in range(nseg):
                    scn = MODS[:, (3 * (l + 1) + 1) * DC:(3 * (l + 1) + 2) * DC, s]
                    shn = MODS[:, (3 * (l + 1)) * DC:(3 * (l + 1) + 1) * DC, s]
                    P.emit("dve", (lambda e, l=l, s=s, scn=scn: e.tensor_tensor(out=BM[:, l, :, s], in0=LNB[:, l, :], in1=scn,
                                                                               op=ALU.mult)),
                           reads=[rMODS, rSETUP], writes=[rBM])
                    P.emit("dve", (lambda e, l=l, s=s, shn=shn: e.tensor_tensor(out=BM[:, l, :, s], in0=BM[:, l, :, s], in1=shn,
                                                                               op=ALU.add)),
                           reads=[rMODS, rBM], writes=[rBM])

            def mod_ap(g, c, s):
                return MODS[:, g * DC + c, s:s + 1]

            stop = int(os.environ.get("K_STOP", "0"))
            a1lvl = int(os.environ.get("K_A1", "9"))
            if stop == 1:
                raise _Stop()
            rWP = Res()
            P.emit("pool", lambda e: e.dma_start(out=WP[:], in_=w_pool.rearrange("g (kc p) j -> p g kc j", p=128)),
                   writes=[rWP], dma_sem="wp")

            cv_i = [0]
            rCVS = [Res() for _ in range(8)]

            def conv_dma(dst, src, piece_res):
                i = cv_i[0] % 8
                cv_i[0] += 1
                ev = P.emit("pool", lambda e: e.dma_start(out=dst, in_=src), writes=[rCVS[i]], dma_sem=f"cv{i}")
                if ev.val > piece_res.W.get(ev.sem, 0):
                    piece_res.W[ev.sem] = ev.val

            def colsrc(w, r0, nk, c0):
                return w[r0:r0 + nk * 128, c0:c0 + 128].rearrange("(kc p) j -> p kc j", p=128)

            def dstv(t, off, nk):
                return t[:, off:off + nk * 128].rearrange("p (kc j) -> p kc j", j=128)

            r13 = {f: [Res() for _ in range(FC)] for f in (1, 2)}
            r2 = {f: [Res() for _ in range(2 * DC)] for f in (1, 2)}
            rin = [Res() for _ in range(16)]
            rout = [Res() for _ in range(8)]
            in_order = []
            for j in range(PC):
                in_order.append((16 + j, 24 + j))
                in_order.append((8 + j, j))
            J_ORDER = list(range(PC - 1, -1, -1))

            def conv_ffn(f):
                w1, w3, w2 = wd[f]
                for fc in range(FC):
                    conv_dma(dstv(S13[f][fc], 0, DC), colsrc(w1, 0, DC, fc * 128), r13[f][fc])
                    conv_dma(dstv(S13[f][fc], 2048, DC), colsrc(w3, 0, DC, fc * 128), r13[f][fc])
                for oc in range(DC):
                    for hf in range(2):
                        conv_dma(dstv(S2[f][2 * oc + hf], 0, KH[hf]), colsrc(w2, hf * KH[0] * 128, KH[hf], oc * 128),
                                 r2[f][2 * oc + hf])
            conv_ffn(1)
            for j in J_ORDER:
                for pc in (2 * j, 2 * j + 1):
                    for e_ in range(2):
                        conv_dma(dstv(Sin[pc], e_ * 2048, DC), colsrc(w_in, 0, DC, in_order[pc][e_] * 128), rin[pc])
            for pc in range(8):
                for e_ in range(2):
                    conv_dma(dstv(Sout[pc], e_ * 2048, DC), colsrc(w_out, 0, DC, (2 * pc + e_) * 128), rout[pc])
            conv_ffn(2)

            if stop == 2:
                raise _Stop()
            def mm_cols(lo, hi):
                r = rBANK.next()
                return r, r.ap[:, 0:hi - lo]

            def ln_accumulate(c, colr, first):
                for (lo, hi) in colr:
                    xa = X[:, c, lo:hi]
                    if first:
                        P.emit("dve", (lambda e, xa=xa, lo=lo, hi=hi: e.tensor_copy(out=ACC[:, lo:hi], in_=xa)),
                               reads=[rX[c]], writes=[rACC])
                        P.emit("act", (lambda e, xa=xa, lo=lo, hi=hi: e.activation(out=ACCSQ[:, lo:hi], in_=xa, func=AF.Square)),
                               reads=[rX[c]], writes=[rACCSQ])
                    else:
                        P.emit("dve", (lambda e, xa=xa, lo=lo, hi=hi: e.tensor_tensor(out=ACC[:, lo:hi], in0=ACC[:, lo:hi], in1=xa,
                                                                                     op=ALU.add)),
                               reads=[rX[c], rACC], writes=[rACC])
                        sq = rSCR.next()
                        n = hi - lo
                        P.emit("act", (lambda e, xa=xa, sq=sq, n=n: e.activation(out=sq.ap[:, 0:n], in_=xa, func=AF.Square)),
                               reads=[rX[c]], writes=[sq])
                        P.emit("dve", (lambda e, sq=sq, lo=lo, hi=hi, n=n: e.tensor_tensor(out=ACCSQ[:, lo:hi], in0=ACCSQ[:, lo:hi],
                                                                                          in1=sq.ap[:, 0:n], op=ALU.add)),
                               reads=[sq, rACCSQ], writes=[rACCSQ])

            def layer_norm(l, s, colr, make_xm, split=False):
                def stats():
                  for (lo, hi) in colr:
                      n = hi - lo
                      rs, aps = mm_cols(lo, hi)
                      rq, apq = mm_cols(lo, hi)
                      P.emit("pe", (lambda e, aps=aps, lo=lo, hi=hi: e.matmul(aps, ONES[:], ACC[:, lo:hi], start=True, stop=True)),
                             reads=[rACC, rSETUP], writes=[rs])
                      P.emit("pe", (lambda e, apq=apq, lo=lo, hi=hi: e.matmul(apq, ONES[:], ACCSQ[:, lo:hi], start=True, stop=True)),
                             reads=[rACCSQ, rSETUP], writes=[rq])
                      P.emit("dve", (lambda e, aps=aps, lo=lo, hi=hi: e.tensor_scalar(out=MEAN[:, lo:hi], in0=aps, scalar1=1.0 / D,
                                                                                     scalar2=None, op0=ALU.mult)),
                             reads=[rs], writes=[rMEAN])
                      m2 = rSCR.next()
                      P.emit("dve", (lambda e, m2=m2, lo=lo, hi=hi, n=n: e.tensor_tensor(out=m2.ap[:, 0:n], in0=MEAN[:, lo:hi],
                                                                                        in1=MEAN[:, lo:hi], op=ALU.mult)),
                             reads=[rMEAN], writes=[m2])
                      P.emit("dve", (lambda e, m2=m2, apq=apq, lo=lo, hi=hi, n=n: e.scalar_tensor_tensor(
                          out=RSTD[:, lo:hi], in0=apq, scalar=1.0 / D, in1=m2.ap[:, 0:n], op0=ALU.mult, op1=ALU.subtract)),
                          reads=[rq, m2], writes=[rRSTD])
                      P.emit("dve", (lambda e, lo=lo, hi=hi: e.tensor_scalar(out=RSTD[:, lo:hi], in0=RSTD[:, lo:hi], scalar1=0.0,
                                                                            scalar2=EPS_P, op0=ALU.max, op1=ALU.add)),
                             reads=[rRSTD], writes=[rRSTD])
                      P.emit("act", (lambda e, lo=lo, hi=hi: e.activation(out=RSTD[:, lo:hi], in_=RSTD[:, lo:hi], func=AF.Sqrt)),
                             reads=[rRSTD], writes=[rRSTD])
                      P.emit("dve", (lambda e, lo=lo, hi=hi: e.reciprocal(out=RSTD[:, lo:hi], in_=RSTD[:, lo:hi])),
                             reads=[rRSTD], writes=[rRSTD])
                def chunk(c):
                    for (lo, hi) in colr:
                        n = hi - lo
                        t = rSCR.next()
                        v = rSCR.next()
                        xa = X[:, c, lo:hi]
                        P.emit("dve", (lambda e, t=t, xa=xa, lo=lo, hi=hi, n=n: e.tensor_tensor(
                            out=t.ap[:, 0:n], in0=xa, in1=MEAN[:, lo:hi], op=ALU.subtract)),
                            reads=[rX[c], rMEAN], writes=[t])
                        P.emit("dve", (lambda e, t=t, v=v, c=c, lo=lo, hi=hi, n=n: e.scalar_tensor_tensor(
                            out=v.ap[:, 0:n], in0=t.ap[:, 0:n], scalar=LNG[:, l, c:c + 1], in1=RSTD[:, lo:hi],
                            op0=ALU.mult, op1=ALU.mult)),
                            reads=[t, rRSTD, rSETUP], writes=[v])
                        P.emit("act", (lambda e, v=v, xa=xa, c=c, n=n: e.activation(
                            out=xa, in_=v.ap[:, 0:n], func=AF.Identity, bias=LNB[:, l, c:c + 1], scale=1.0)),
                            reads=[v, rSETUP], writes=[rX[c]])
                        if make_xm:
                            xma = XM[:, c, lo:hi]
                            P.emit("act", (lambda e, v=v, xma=xma, c=c, n=n: e.activation(
                                out=xma, in_=v.ap[:, 0:n], func=AF.Identity, bias=BM[:, l, c, s:s + 1],
                                scale=mod_ap(3 * (l + 1) + 1, c, s))),
                                reads=[v, rBM, rMODS], writes=[rXM[c]])

                if split:
                    return stats, chunk
                stats()
                for c in range(DC):
                    chunk(c)

            def ffn(f, s, colr, gidx, between=None, up_hooks=None):
                def up_evac(fc, lo, hi, ra, apa, rb, apb):
                    n = hi - lo
                    sa = rSCR.next()
                    P.emit("act", (lambda e, sa=sa, apa=apa, n=n: e.activation(out=sa.ap[:, 0:n], in_=apa, func=AF.Silu)),
                           reads=[ra], writes=[sa])
                    P.emit("dve", (lambda e, sa=sa, apb=apb, fc=fc, lo=lo, hi=hi, n=n: e.tensor_tensor(
                        out=Hv[:, fc, lo:hi], in0=sa.ap[:, 0:n], in1=apb, op=ALU.mult)),
                        reads=[sa, rb], writes=[rH[fc]])

                def up_fc(fc, slot_r, slot, lo, hi):
                    ra, apa = mm_cols(lo, hi)
                    rb, apb = mm_cols(lo, hi)

                    def fn(e, slot=slot, apa=apa, apb=apb, lo=lo, hi=hi):
                        ins = None
                        for kc in range(DC):
                            ins = e.matmul(apa, slot[:, kc * 128:(kc + 1) * 128], XM[:, kc, lo:hi],
                                           start=(kc == 0), stop=(kc == DC - 1))
                        for kc in range(DC):
                            ins = e.matmul(apb, slot[:, 2048 + kc * 128:2048 + (kc + 1) * 128], XM[:, kc, lo:hi],
                                           start=(kc == 0), stop=(kc == DC - 1))
                        return ins
                    P.emit("pe", fn, reads=[slot_r] + rXM, writes=[ra, rb])
                    up_evac(fc, lo, hi, ra, apa, rb, apb)

                G = 3
                mlo, mhi = colr[0]
                grp = []
                for fc in range(G):
                    slot_r, slot = ring_load(S13[f][fc], SLOT, r13[f][fc])
                    ra, apa = mm_cols(mlo, mhi)
                    rb, apb = mm_cols(mlo, mhi)
                    grp.append((fc, slot_r, slot, ra, apa, rb, apb))
                for kc in range(DC):
                    def fn(e, kc=kc):
                        ins = None
                        for (fc, slot_r, slot, ra, apa, rb, apb) in grp:
                            ins = e.matmul(apa, slot[:, kc * 128:(kc + 1) * 128], XM[:, kc, mlo:mhi],
                                           start=(kc == 0), stop=(kc == DC - 1))
                            ins = e.matmul(apb, slot[:, 2048 + kc * 128:2048 + (kc + 1) * 128], XM[:, kc, mlo:mhi],
                                           start=(kc == 0), stop=(kc == DC - 1))
                        return ins
                    P.emit("pe", fn, reads=[g_[1] for g_ in grp] + [rXM[kc]],
                           writes=[g_[3] for g_ in grp] + [g_[5] for g_ in grp])
                for (fc, slot_r, slot, ra, apa, rb, apb) in grp:
                    up_evac(fc, mlo, mhi, ra, apa, rb, apb)
                    for (lo, hi) in colr[1:]:
                        up_fc(fc, slot_r, slot, lo, hi)
                for fc in range(G, FC):
                    slot_r, slot = ring_load(S13[f][fc], SLOT, r13[f][fc])
                    for (lo, hi) in colr:
                        up_fc(fc, slot_r, slot, lo, hi)
                    if up_hooks and fc in up_hooks:
                        up_hooks[fc]()
                if between is not None:
                    between()
                for oc in range(DC):
                    dests = [mm_cols(lo, hi) for (lo, hi) in colr]
                    for hf in range(2):
                        nk = KH[hf]
                        slot_r, slot = ring_load(S2[f][2 * oc + hf][:, 0:nk * 128], nk * 128, r2[f][2 * oc + hf])
                        for ci, (lo, hi) in enumerate(colr):
                            rd, apd = dests[ci]

                            def fn(e, slot=slot, apd=apd, lo=lo, hi=hi, hf=hf, nk=nk):
                                ins = None
                                for k in range(nk):
                                    kc = hf * KH[0] + k
                                    ins = e.matmul(apd, slot[:, k * 128:(k + 1) * 128], Hv[:, kc, lo:hi],
                                                   start=(kc == 0), stop=(kc == FC - 1))
                                return ins
                            P.emit("pe", fn, reads=[slot_r] + rH[hf * KH[0]:hf * KH[0] + nk], writes=[rd])
                    for ci, (lo, hi) in enumerate(colr):
                        rd, apd = dests[ci]
                        xa = X[:, oc, lo:hi]
                        P.emit("dve", (lambda e, apd=apd, xa=xa, oc=oc: e.scalar_tensor_tensor(
                            out=xa, in0=apd, scalar=mod_ap(gidx, oc, s), in1=xa, op0=ALU.mult, op1=ALU.add)),
                            reads=[rd, rX[oc], rMODS], writes=[rX[oc]])
                    ln_accumulate(oc, colr, oc == 0)

            BW = (8, 8 + TT)
            tile_id = 0
            xrow = 0
            yst_i = 0
            xst_i = 0
            tiles = []
            xr_ = 0
            for s_, ntl_ in enumerate(segs):
                for i_ in range(ntl_):
                    tiles.append((s_, i_, xr_ + 16 + TT * i_))
                xr_ += ntl_ * TT + 16
            xpre = {}
            xst_c = [0]

            def xload(tid, fg, tb):
                key = (tid, fg, tb)
                if key in xpre:
                    return xpre.pop(key)
                k = xst_c[0] % 4
                xst_c[0] += 1
                st = rXST[k]
                rm = tiles[tid][2]
                src = xs[rm + tb * 128:rm + (tb + 1) * 128, fg * 512:(fg + 1) * 512]
                P.emit("sp" if tid == 0 else "pool", (lambda e, st=st, src=src: e.dma_start(out=st.ap, in_=src)),
                       writes=[st], dma_sem=(f"xst{k}" if tid == 0 else f"xsw{k}"))
                return st

            def in_stage(tid, mode):
                s_t, i_t, rm = tiles[tid]
                first_t = (i_t == 0)
                xrow_t = rm - 16 - TT * i_t
                q = "sp" if tid == 0 else "pool"
                ebank = rEBANK
                for fg in range(4):
                    cb = [rBANK.next() for _ in range(4)]
                    for tb in range(4):
                        st = xload(tid, fg, tb)

                        def fn(e, st=st, cb=cb, tb=tb):
                            ins = None
                            for c4 in range(4):
                                ins = e.transpose(cb[c4].ap[:, tb * 128:(tb + 1) * 128], st.ap[:, c4 * 128:(c4 + 1) * 128],
                                                  IDENT[:])
                            return ins
                        P.emit("pe", fn, reads=[st, rSETUP], writes=cb)
                    if first_t:
                        k_ = xst_c[0] % 4
                        xst_c[0] += 1
                        st = rXST[k_]
                        src = xs[xrow_t:xrow_t + 16, fg * 512:(fg + 1) * 512]
                        P.emit(q, (lambda e, st=st, src=src: e.dma_start(out=st.ap[0:16, :], in_=src)), writes=[st],
                               dma_sem=(f"xst{k_}" if tid == 0 else f"xsw{k_}"))

                        def fn(e, st=st, fg=fg):
                            ins = None
                            for c4 in range(4):
                                c = fg * 4 + c4
                                ins = e.transpose(ebank.ap[:, c * 16:(c + 1) * 16], st.ap[0:16, c4 * 128:(c4 + 1) * 128],
                                                  IDENT[0:16, 0:16])
                            return ins
                        P.emit("pe", fn, reads=[st, rSETUP], writes=[ebank])
                    for c4 in range(4):
                        c = fg * 4 + c4
                        bk = cb[c4]
                        if mode in ("both", "x"):
                            P.emit("act", (lambda e, bk=bk, c=c: e.activation(out=X[:, c, 16:FW], in_=bk.ap[:, 0:TT], func=AF.Copy)),
                                   reads=[bk], writes=[rX[c]])
                        if mode == "both":
                            P.emit("dve", (lambda e, c=c, s_t=s_t: e.tensor_scalar(
                                out=XM[:, c, 16:FW], in0=X[:, c, 16:FW], scalar1=mod_ap(1, c, s_t), scalar2=mod_ap(0, c, s_t),
                                op0=ALU.mult, op1=ALU.add)),
                                reads=[rX[c], rMODS], writes=[rXM[c]])
                        if mode == "xm":
                            P.emit("act", (lambda e, bk=bk, c=c, s_t=s_t: e.activation(
                                out=XM[:, c, 16:FW], in_=bk.ap[:, 0:TT], func=AF.Identity,
                                bias=mod_ap(0, c, s_t), scale=mod_ap(1, c, s_t))),
                                reads=[bk, rMODS], writes=[rXM[c]])
                if first_t:
                    if mode in ("both", "x"):
                        P.emit("act", (lambda e: e.activation(
                            out=X[:, :, 0:16], in_=ebank.ap[:, 0:256].rearrange("p (c k) -> p c k", k=16), func=AF.Copy)),
                            reads=[ebank], writes=rX)
                    for c in range(DC):
                        if mode == "both":
                            P.emit("dve", (lambda e, c=c, s_t=s_t: e.tensor_scalar(
                                out=XM[:, c, 0:16], in0=X[:, c, 0:16], scalar1=mod_ap(1, c, s_t),
                                scalar2=mod_ap(0, c, s_t), op0=ALU.mult, op1=ALU.add)),
                                reads=[rX[c], rMODS], writes=[rXM[c]])
                        if mode == "xm":
                            P.emit("act", (lambda e, c=c, s_t=s_t: e.activation(
                                out=XM[:, c, 0:16], in_=ebank.ap[:, c * 16:(c + 1) * 16], func=AF.Identity,
                                bias=mod_ap(0, c, s_t), scale=mod_ap(1, c, s_t))),
                                reads=[ebank, rMODS], writes=[rXM[c]])

            yst_c = [0]
            deferred = [None]

            for s, ntl in enumerate(segs):
                for i in range(ntl):
                    first = (i == 0)
                    last = (i == ntl - 1)
                    acols = [(16, FW)] + ([(0, 16)] if first else [])
                    bcols = [BW]
                    row_main = xrow + 16 + TT * i
                    P.emit("sp", (lambda e, tile_id=tile_id: e.dma_start(out=INVC[:], in_=invc[tile_id])),
                           writes=[rINVC], dma_sem="invc")

                    if tile_id == 0:
                        in_stage(0, "both")

                    if stop == 3:
                        raise _Stop()
                    hooks = None
                    if tile_id > 0:
                        d_stats, d_chunk, d_out = deferred[0]
                        hooks = {3: d_stats, 12: d_out, 16: (lambda tile_id=tile_id: in_stage(tile_id, "x"))}
                        for k_ in range(8):
                            hooks[4 + k_] = (lambda k_=k_, d_chunk=d_chunk: (d_chunk(2 * k_), d_chunk(2 * k_ + 1)))
                    ffn(1, s, acols, 2, up_hooks=hooks)
                    layer_norm(0, s, acols, True)

                    if stop == 4:
                        raise _Stop()
                    def in_group(pc, colr, kc_major=False):
                        slot_r, slot = ring_load(Sin[pc], SLOT, rin[pc])
                        outs = []
                        if kc_major:
                            mlo, mhi = colr[0]
                            dm = [mm_cols(mlo, mhi) for _ in range(2)]
                            for kc in range(DC):
                                def fn(e, kc=kc, slot=slot, dm=dm, mlo=mlo, mhi=mhi):
                                    ins = None
                                    for e_ in range(2):
                                        ins = e.matmul(dm[e_][1], slot[:, e_ * 2048 + kc * 128:e_ * 2048 + (kc + 1) * 128],
                                                       XM[:, kc, mlo:mhi], start=(kc == 0), stop=(kc == DC - 1))
                                    return ins
                                P.emit("pe", fn, reads=[slot_r, rXM[kc]], writes=[dm[0][0], dm[1][0]])
                        for e_ in range(2):
                            dd = []
                            for ci, (lo, hi) in enumerate(colr):
                                if kc_major and ci == 0:
                                    dd.append(dm[e_])
                                    continue
                                rd, apd = mm_cols(lo, hi)

                                def fn(e, slot=slot, apd=apd, lo=lo, hi=hi, e_=e_):
                                    ins = None
                                    for kc in range(DC):
                                        ins = e.matmul(apd, slot[:, e_ * 2048 + kc * 128:e_ * 2048 + (kc + 1) * 128],
                                                       XM[:, kc, lo:hi], start=(kc == 0), stop=(kc == DC - 1))
                                    return ins
                                P.emit("pe", fn, reads=[slot_r] + rXM, writes=[rd])
                                dd.append((rd, apd))
                            outs.append(dd)
                        return outs

                    def mask_edges(tl):
                        if first:
                            P.emit("dve", (lambda e, tl=tl, s=s: e.tensor_tensor(out=tl.ap[:, 0:8], in0=tl.ap[:, 0:8],
                                                                                in1=VM[:, s * 16:s * 16 + 8], op=ALU.mult)),
                                   reads=[tl, rSETUP], writes=[tl])
                        if last:
                            P.emit("dve", (lambda e, tl=tl, s=s: e.tensor_tensor(out=tl.ap[:, 520:528], in0=tl.ap[:, 520:528],
                                                                                in1=VM[:, s * 16 + 8:s * 16 + 16], op=ALU.mult)),
                                   reads=[tl, rSETUP], writes=[tl])

                    pending_wp = []
                    for j in J_ORDER:
                        (dC, dV) = in_group(2 * j, acols, kc_major=(j == J_ORDER[0]))
                        while pending_wp:
                            pending_wp.pop(0)()
                        zt = rSCR.next()
                        for ci, (lo, hi) in enumerate(acols):
                            n = hi - lo
                            cs = rSCR.next()
                            rc, apc = dC[ci]
                            rv, apv = dV[ci]
                            P.emit("act", (lambda e, cs=cs, apc=apc, n=n: e.activation(out=cs.ap[:, 0:n], in_=apc, func=AF.Copy)),
                                   reads=[rc], writes=[cs])
                            P.emit("dve", (lambda e, cs=cs, apv=apv, zt=zt, lo=lo, hi=hi, n=n: e.tensor_tensor(
                                out=zt.ap[:, lo:hi], in0=cs.ap[:, 0:n], in1=apv, op=ALU.mult)),
                                reads=[cs, rv], writes=[zt])
                        if not first:
                            P.emit("act", (lambda e, zt=zt, j=j, s=s: e.activation(out=zt.ap[:, 0:16], in_=CZ[:, s, j, :], func=AF.Copy)),
                                   reads=[rCZ], writes=[zt])
                        mask_edges(zt)
                        if not last:
                            P.emit("act", (lambda e, zt=zt, j=j, s=s: e.activation(out=CZ[:, s, j, :], in_=zt.ap[:, 512:528], func=AF.Copy)),
                                   reads=[zt], writes=[rCZ])
                        t1 = rSCR.next()
                        P.emit("dve", (lambda e, zt=zt, t1=t1, j=j: e.tensor_scalar(
                            out=t1.ap[:, 0:TT], in0=zt.ap[:, 7:7 + TT], scalar1=WCV[:, 0, j:j + 1], scalar2=None, op0=ALU.mult)),
                            reads=[zt, rSETUP], writes=[t1])
                        P.emit("dve", (lambda e, zt=zt, t1=t1, j=j: e.scalar_tensor_tensor(
                            out=t1.ap[:, 0:TT], in0=zt.ap[:, 8:8 + TT], scalar=WCV[:, 1, j:j + 1], in1=t1.ap[:, 0:TT],
                            op0=ALU.mult, op1=ALU.add)),
                            reads=[zt, t1, rSETUP], writes=[t1])
                        P.emit("dve", (lambda e, zt=zt, t1=t1, j=j: e.scalar_tensor_tensor(
                            out=t1.ap[:, 0:TT], in0=zt.ap[:, 9:9 + TT], scalar=WCV[:, 2, j:j + 1], in1=t1.ap[:, 0:TT],
                            op0=ALU.mult, op1=ALU.add)),
                            reads=[zt, t1, rSETUP], writes=[t1])
                        (dB, dU) = in_group(2 * j + 1, acols)
                        rb_, apb_ = dB[0]
                        P.emit("dve", (lambda e, t1=t1, apb_=apb_, j=j: e.tensor_tensor(
                            out=HEADS[:, PC + j, 8:TT], in0=t1.ap[:, 8:TT], in1=apb_[:, 0:TT - 8], op=ALU.mult)),
                            reads=[t1, rb_], writes=[rHEADS[PC + j]])
                        if first:
                            rbx, apbx = dB[1]
                            P.emit("dve", (lambda e, t1=t1, apbx=apbx, j=j: e.tensor_tensor(
                                out=HEADS[:, PC + j, 0:8], in0=t1.ap[:, 0:8], in1=apbx[:, 8:16], op=ALU.mult)),
                                reads=[t1, rbx], writes=[rHEADS[PC + j]])
                        else:
                            P.emit("dve", (lambda e, t1=t1, j=j, s=s: e.tensor_tensor(
                                out=HEADS[:, PC + j, 0:8], in0=t1.ap[:, 0:8], in1=CB[:, s, j, :], op=ALU.mult)),
                                reads=[t1, rCB], writes=[rHEADS[PC + j]])
                        if not last:
                            P.emit("dve", (lambda e, apb_=apb_, j=j, s=s: e.tensor_copy(out=CB[:, s, j, :], in_=apb_[:, TT - 8:TT])),
                               reads=[rb_], writes=[rCB])
                        g = j // 2
                        ut = rSCR.next()
                        for ci, (lo, hi) in enumerate(acols):
                            ru, apu = dU[ci]
                            P.emit("act", (lambda e, ut=ut, apu=apu, lo=lo, hi=hi: e.activation(out=ut.ap[:, lo:hi], in_=apu,
                                                                                              func=AF.Copy)),
                                   reads=[ru], writes=[ut])
                        if not first:
                            P.emit("act", (lambda e, ut=ut, j=j, s=s: e.activation(out=ut.ap[:, 0:16], in_=CU[:, s, j, :], func=AF.Copy)),
                                   reads=[rCU], writes=[ut])
                        mask_edges(ut)
                        if not last:
                            P.emit("act", (lambda e, ut=ut, j=j, s=s: e.activation(out=CU[:, s, j, :], in_=ut.ap[:, 512:528], func=AF.Copy)),
                                   reads=[ut], writes=[rCU])
                        cur = ut
                        lo_c, hi_c = 0, FW
                        half = 1
                        first_step = True
                        for _ in range(g + 1):
                            nxt = rSCR.next()
                            if first_step:
                                nlo, nhi = lo_c + 1, hi_c
                                a0, a1 = nlo - 1, nlo
                                first_step = False
                            else:
                                h_ = half // 2
                                nlo, nhi = lo_c + h_, hi_c - h_
                                a0, a1 = nlo - h_, nlo + h_
                            nn = nhi - nlo
                            P.emit("dve", (lambda e, cur=cur, nxt=nxt, a0=a0, a1=a1, nlo=nlo, nn=nn: e.tensor_tensor(
                                out=nxt.ap[:, nlo:nlo + nn], in0=cur.ap[:, a0:a0 + nn], in1=cur.ap[:, a1:a1 + nn], op=ALU.add)),
                                reads=[cur], writes=[nxt])
                            cur = nxt
                            lo_c, hi_c = nlo, nhi
                            half *= 2
                        pm = rSCR.next()
                        P.emit("dve", (lambda e, cur=cur, pm=pm, g=g: e.tensor_tensor(
                            out=pm.ap[:, 0:TT], in0=cur.ap[:, 8:8 + TT], in1=INVC[:, g * TT:(g + 1) * TT], op=ALU.mult)),
                            reads=[cur, rINVC], writes=[pm])
                        P.emit("dve", (lambda e, pm=pm, ut=ut, j=j: e.tensor_tensor(
                            out=POOLED[:, j, :], in0=pm.ap[:, 0:TT], in1=ut.ap[:, 8:8 + TT], op=ALU.subtract)),
                            reads=[pm, ut], writes=[rPOOLED[j]])
                        if j % 2 == 0:
                            def wp_emit(g=g):
                                for o2 in range(2):
                                    rd, apd = mm_cols(0, TT)

                                    def fn(e, apd=apd, g=g, o2=o2):
                                        ins = None
                                        for k2 in range(2):
                                            ins = e.matmul(apd, WP[:, g, k2, o2 * 128:(o2 + 1) * 128], POOLED[:, 2 * g + k2, :],
                                                           start=(k2 == 0), stop=(k2 == 1))
                                        return ins
                                    P.emit("pe", fn, reads=[rWP, rPOOLED[2 * g], rPOOLED[2 * g + 1]], writes=[rd])
                                    hc = 2 * g + o2
                                    P.emit("act", (lambda e, apd=apd, hc=hc: e.activation(out=HEADS[:, hc, :], in_=apd,
                                                                                         func=AF.Identity, scale=SP_[:, hc:hc + 1])),
                                           reads=[rd, rSETUP], writes=[rHEADS[hc]])
                            pending_wp.append(wp_emit)
                    while pending_wp:
                        pending_wp.pop(0)()

                    if stop == 5:
                        raise _Stop()
                    for pc in range(8):
                        slot_r, slot = ring_load(Sout[pc], SLOT, rout[pc])
                        for e_ in range(2):
                            oc = 2 * pc + e_
                            rd, apd = mm_cols(0, TT)

                            def fn(e, slot=slot, apd=apd, e_=e_):
                                ins = None
                                for kc in range(DC):
                                    ins = e.matmul(apd, slot[:, e_ * 2048 + kc * 128:e_ * 2048 + (kc + 1) * 128], HEADS[:, kc, :],
                                                   start=(kc == 0), stop=(kc == DC - 1))
                                return ins
                            P.emit("pe", fn, reads=[slot_r] + rHEADS, writes=[rd])
                            xa = X[:, oc, BW[0]:BW[1]]
                            P.emit("dve", (lambda e, apd=apd, xa=xa, oc=oc, s=s: e.scalar_tensor_tensor(
                                out=xa, in0=apd, scalar=mod_ap(5, oc, s), in1=xa, op0=ALU.mult, op1=ALU.add)),
                                reads=[rd, rX[oc], rMODS], writes=[rX[oc]])
                            ln_accumulate(oc, bcols, oc == 0)
                    layer_norm(1, s, bcols, True)

                    if stop == 6:
                        raise _Stop()
                    if tile_id + 1 < len(tiles):
                        for tb_ in range(4):
                            st_ = xload(tile_id + 1, 0, tb_)
                            xpre[(tile_id + 1, 0, tb_)] = st_
                    nxt = tile_id + 1 if tile_id + 1 < len(tiles) else None
                    ffn(2, s, bcols, 8, between=((lambda nxt=nxt: in_stage(nxt, "xm")) if nxt is not None else None))
                    ln3_stats, ln3_chunk = layer_norm(2, s, bcols, False, split=True)
                    def out_stage(tile_id=tile_id, last=last):
                        for tb in range(4):
                            for fg in range(4):
                                bk = rBANK.next()

                                def fn(e, bk=bk, tb=tb, fg=fg):
                                    ins = None
                                    for c4 in range(4):
                                        c = fg * 4 + c4
                                        ins = e.transpose(bk.ap[:, c4 * 128:(c4 + 1) * 128],
                                                          X[:, c, 8 + tb * 128:8 + (tb + 1) * 128], IDENT[:])
                                    return ins
                                P.emit("pe", fn, reads=rX[fg * 4:fg * 4 + 4] + [rSETUP], writes=[bk])
                                ks = yst_c[0] % 4
                                yst_c[0] += 1
                                st = rYST[ks]
                                sem = f"yst{ks}"
                                if (tb * 4 + fg) % 2 == 0:
                                    P.emit("act", (lambda e, st=st, bk=bk: e.activation(out=st.ap, in_=bk.ap[:, 0:512], func=AF.Copy)),
                                           reads=[bk], writes=[st])
                                else:
                                    P.emit("dve", (lambda e, st=st, bk=bk: e.tensor_copy(out=st.ap, in_=bk.ap[:, 0:512])),
                                           reads=[bk], writes=[st])
                                r0 = tile_id * TT + tb * 128
                                dst = y[r0:r0 + 128, fg * 512:(fg + 1) * 512]
                                P.emit("pool", (lambda e, st=st, dst=dst: e.dma_start(out=dst, in_=st.ap)), reads=[st], dma_sem=sem)
                        if not last:
                            P.emit("dve", lambda e: e.tensor_copy(out=X[:, :, 8:16], in_=X[:, :, 520:528]), reads=rX, writes=rX)

                    if nxt is not None:
                        deferred[0] = (ln3_stats, ln3_chunk, out_stage)
                    else:
                        ln3_stats()
                        for c_ in range(DC):
                            ln3_chunk(c_)
                        out_stage()
                    tile_id += 1
                xrow += ntl * TT + 16

        try:
            emit_all()
        except _Stop:
            pass

        final_waits = [(f"yst{k}", P.cnt.get(f"yst{k}", 0)) for k in range(4)]

        def replay(e, eng, extra_waits=()):
            for waits, fn, ev, is_dma in P.ops.get(eng, []):
                for (sname, v) in waits:
                    e.wait_ge(SEM[sname], v)
                ins = fn(e)
                ins.then_inc(SEM[ev.sem], 16 if is_dma else 1)
            for (sname, v) in extra_waits:
                if v > 0:
                    e.wait_ge(SEM[sname], v)

        @block.sync
        def _(e):
            replay(e, "sp")

        @block.gpsimd
        def _(e):
            replay(e, "pool", final_waits)

        @block.tensor
        def _(e):
            replay(e, "pe")

        @block.scalar
        def _(e):
            replay(e, "act")

        @block.vector
        def _(e):
            replay(e, "dve")
    return nc


def _feat_major(v):
    v = np.asarray(v, dtype=np.float32)
    return np.ascontiguousarray(v.reshape(-1, 128).T)


def _invcnt(pos, seqlen):
    out = np.empty((4, pos.shape[0]), dtype=np.float32)
    for g, w in enumerate(WINDOWS):
        lo = np.clip(pos - w // 2, 0, seqlen)
        hi = np.clip(pos + w // 2, 0, seqlen)
        out[g] = 1.0 / (hi - lo).astype(np.float32)
    return out


def make_core_inputs(seg_descs, c_list, shared):
    rows = []
    invs = []
    vm = np.zeros((128, len(seg_descs) * 16), dtype=np.float32)
    for si, (xseq, start, ntl) in enumerate(seg_descs):
        L = xseq.shape[0]
        n = ntl * TT
        lo, hi = start - 8, start + n + 8
        seg = np.zeros((n + 16, D), dtype=np.float32)
        a, b = max(lo, 0), min(hi, L)
        seg[a - lo:b - lo] = xseq[a:b]
        rows.append(seg)
        vm[:, si * 16:si * 16 + 8] = 1.0 if lo >= 0 else 0.0
        vm[:, si * 16 + 8:si * 16 + 16] = 1.0 if hi <= L else 0.0
        for i in range(ntl):
            pos = start + i * TT + np.arange(TT)
            iv = _invcnt(pos, L).reshape(1, 4 * TT)
            invs.append(np.broadcast_to(iv, (128, 4 * TT)))
    nseg = len(seg_descs)
    cT = np.empty((128, DC * nseg), dtype=np.float32)
    for si, cv in enumerate(c_list):
        cT[:, si::nseg] = _feat_major(cv)
    m = dict(shared)
    m["xs"] = np.ascontiguousarray(np.concatenate(rows, axis=0))
    m["cT"] = cT
    m["invc"] = np.ascontiguousarray(np.stack(invs, axis=0))
    m["vmask"] = vm
    return m


def make_shared(w_ada, b_ada, ffn1_w1, ffn1_w3, ffn1_w2, w_in, w_pool, s_pool, w_conv, w_out,
                ffn2_w1, ffn2_w3, ffn2_w2, ln_g, ln_b):
    f32 = lambda a: np.ascontiguousarray(np.asarray(a, dtype=np.float32))
    small = np.concatenate([
        _feat_major(s_pool[0]),
        np.concatenate([_feat_major(w_conv[0][k]) for k in range(3)], axis=1),
        np.concatenate([_feat_major(ln_g[0][l]) for l in range(3)], axis=1),
        np.concatenate([_feat_major(ln_b[0][l]) for l in range(3)], axis=1),
    ], axis=1)
    return {
        "w_ada": f32(w_ada[0]), "b_adaT": _feat_major(b_ada[0]),
        "f1w1": f32(ffn1_w1[0]), "f1w3": f32(ffn1_w3[0]), "f1w2": f32(ffn1_w2[0]),
        "f2w1": f32(ffn2_w1[0]), "f2w3": f32(ffn2_w3[0]), "f2w2": f32(ffn2_w2[0]),
        "w_in": f32(w_in[0]), "w_pool": f32(w_pool[0]), "w_out": f32(w_out[0]),
        "smallT": np.ascontiguousarray(small.astype(np.float32)),
        "ident": np.eye(128, dtype=np.float32),
    }


def kernel(x_prompt, x_sample, c_prompt, c_sample, w_ada, b_ada, ffn1_w1, ffn1_w3, ffn1_w2,
           w_in, w_pool, s_pool, w_conv, w_out, ffn2_w1, ffn2_w3, ffn2_w2, ln_g, ln_b):
    x_prompt = np.asarray(x_prompt, dtype=np.float32)
    x_sample = np.asarray(x_sample, dtype=np.float32)
    c_prompt = np.asarray(c_prompt, dtype=np.float32)
    c_sample = np.asarray(c_sample, dtype=np.float32)
    shared = make_shared(*[np.asarray(a) for a in (w_ada, b_ada, ffn1_w1, ffn1_w3, ffn1_w2, w_in, w_pool, s_pool,
                                                   w_conv, w_out, ffn2_w1, ffn2_w3, ffn2_w2, ln_g, ln_b)])
    segs = [4, 8]
    in_maps = []
    for c in range(8):
        q, hh = c // 2, c % 2
        in_maps.append(make_core_inputs(
            [(x_prompt[c], 0, 4), (x_sample[q], hh * 4096, 8)], [c_prompt[c], c_sample[q]], shared))
    nc = build_program(segs)
    res = run_bass_kernel_spmd(nc, in_maps, core_ids=list(range(8)))
    y_prompt = np.empty_like(x_prompt)
    y_sample = np.empty_like(x_sample)
    for c in range(8):
        q, hh = c // 2, c % 2
        yc = res.results[c]["y"]
        y_prompt[c] = yc[0:2048]
        y_sample[q, hh * 4096:(hh + 1) * 4096] = yc[2048:6144]
    return (y_prompt, y_sample)
```
